# Optimizing a Trainium2 kernel written in Bass

```python
import math
import jax, jax.numpy as jnp
from jax import lax
import numpy as np


D_MODEL = 1024
BATCH = 4
SEQ = 4096
DEPTH = 2
DEC_BATCH = 32
DEC_SEQ = 16
PAST_LEN = 1024

CHUNK = 64
EPS = 1e-5
D_MIX = D_MODEL
GROUP_WIDTH = D_MIX // 4

SSM_INNER = GROUP_WIDTH
SSM_HEAD_DIM = 64
SSM_HEADS = SSM_INNER // SSM_HEAD_DIM
SSM_STATE = 128
SSM_CONV = 4
SSM_CONV_DIM = SSM_INNER + 2 * SSM_STATE

SWA_HEAD_DIM = 64
SWA_HEADS = GROUP_WIDTH // SWA_HEAD_DIM
SWA_KV_HEADS = 2
SWA_GROUP = SWA_HEADS // SWA_KV_HEADS
WINDOW = 128
WIN_CHUNKS = WINDOW // CHUNK

SC_WIDTH = GROUP_WIDTH
SC_CONV = 3

CC_WIDTH = GROUP_WIDTH
CC_CONV = 31

D_FF = ((8 * D_MODEL + 3 * 256 - 1) // (3 * 256)) * 256

IN_SIZES = (SSM_INNER, SSM_CONV_DIM, SSM_HEADS,
            SWA_HEADS * SWA_HEAD_DIM, SWA_KV_HEADS * SWA_HEAD_DIM, SWA_KV_HEADS * SWA_HEAD_DIM,
            3 * SC_WIDTH, 2 * CC_WIDTH)
IN_SPLITS = tuple(int(v) for v in np.cumsum(IN_SIZES)[:-1])
D_IN = int(sum(IN_SIZES))

kernel_name = 'hybrid_ssd_swa_conv_stream_step'


def rms_norm(x, g):
    xf = x.astype(jnp.float32)
    y = xf * lax.rsqrt(jnp.mean(xf * xf, axis=-1, keepdims=True) + EPS)
    return (y * g.astype(jnp.float32)).astype(x.dtype)


def layer_norm(x, g, b):
    xf = x.astype(jnp.float32)
    xc = xf - jnp.mean(xf, axis=-1, keepdims=True)
    var = jnp.mean(xc * xc, axis=-1, keepdims=True)
    return (xc * lax.rsqrt(var + EPS) * g.astype(jnp.float32) + b.astype(jnp.float32)).astype(x.dtype)


def causal_dwconv(u, buf, w, b=None):
    width, ch = w.shape
    full = jnp.concatenate([buf.astype(u.dtype), u], axis=1)
    y = lax.conv_general_dilated(full, w.astype(u.dtype)[:, None, :], window_strides=(1,), padding='VALID',
                                 dimension_numbers=('NWC', 'WIO', 'NWC'), feature_group_count=ch)
    if b is not None:
        y = y + b.astype(u.dtype)
    return y, full[:, full.shape[1] - (width - 1):]


def ssd_scan(x, dt, A, Bm, Cm, h0, block):
    f32 = jnp.float32
    n, L, H, P = x.shape
    N = Bm.shape[-1]
    nc = L // block
    xc = x.reshape(n, nc, block, H, P).astype(f32)
    dtc = dt.reshape(n, nc, block, H).astype(f32)
    Bc = Bm.reshape(n, nc, block, N).astype(f32)
    Cc = Cm.reshape(n, nc, block, N).astype(f32)
    acum = jnp.cumsum(dtc * A.astype(f32), axis=2)
    seg = acum[:, :, :, None, :] - acum[:, :, None, :, :]
    causal = jnp.tril(jnp.ones((block, block), dtype=bool))[None, None, :, :, None]
    decay = jnp.exp(jnp.where(causal, seg, -jnp.inf))
    xdt = xc * dtc[..., None]
    cb = jnp.einsum('bcis,bcjs->bcij', Cc, Bc)
    y_intra = jnp.einsum('bcij,bcijh,bcjhp->bcihp', cb, decay, xdt)
    to_end = jnp.exp(acum[:, :, -1:, :] - acum)
    blk_states = jnp.einsum('bcjs,bcjh,bcjhp->bchps', Bc, to_end, xdt)
    blk_decay = jnp.exp(acum[:, :, -1, :])

    def step(h, inp):
        d, s = inp
        return d[:, :, None, None] * h + s, h

    h_last, h_starts = lax.scan(step, h0.astype(f32),
                                (jnp.moveaxis(blk_decay, 1, 0), jnp.moveaxis(blk_states, 1, 0)))
    h_starts = jnp.moveaxis(h_starts, 0, 1)
    y_inter = jnp.einsum('bcis,bchps,bcih->bcihp', Cc, h_starts, jnp.exp(acum))
    return (y_intra + y_inter).reshape(n, L, H, P), h_last


def mixer_ssd(z, xbc, dt_raw, conv_buf, h0, conv_w, conv_b, dt_bias, a_log, d_skip, norm_g, block):
    f32 = jnp.float32
    n, L, _ = z.shape
    xbc, new_buf = causal_dwconv(xbc, conv_buf, conv_w, conv_b)
    xbc = jax.nn.silu(xbc)
    xs, Bm, Cm = jnp.split(xbc, [SSM_INNER, SSM_INNER + SSM_STATE], axis=-1)
    dt = jax.nn.softplus(dt_raw.astype(f32) + dt_bias.astype(f32))
    A = -jnp.exp(a_log.astype(f32))
    xh = xs.reshape(n, L, SSM_HEADS, SSM_HEAD_DIM)
    y, h_last = ssd_scan(xh, dt, A, Bm, Cm, h0, block)
    y = y + d_skip.astype(f32)[:, None] * xh.astype(f32)
    y = y.reshape(n, L, SSM_INNER) * jax.nn.silu(z.astype(f32))
    y = rms_norm(y, norm_g)
    return y.astype(z.dtype), new_buf, h_last.astype(h0.dtype)


def alibi_slopes(n):
    return 2.0 ** (-8.0 * jnp.arange(1, n + 1, dtype=jnp.float32) / n)


def banded_attention(q, k, v, qpos, kpos, sinks):
    f32 = jnp.float32
    s = jnp.einsum('ncikgd,ncjkd->nckgij', q.astype(f32), k.astype(f32)) * (SWA_HEAD_DIM ** -0.5)
    slopes = alibi_slopes(SWA_HEADS).reshape(SWA_KV_HEADS, SWA_GROUP)
    dist = jnp.abs(qpos[:, :, None] - kpos[:, None, :]).astype(f32)
    s = s - slopes[None, None, :, :, None, None] * dist[None, :, None, None]
    qc, kc = qpos // CHUNK, kpos // CHUNK
    valid = ((kpos[:, None, :] >= 0) & (kc[:, None, :] <= qc[:, :, None])
             & (kc[:, None, :] >= qc[:, :, None] - WIN_CHUNKS))
    s = jnp.where(valid[None, :, None, None], s, -jnp.inf)
    sink = jnp.broadcast_to(sinks.astype(f32).reshape(SWA_KV_HEADS, SWA_GROUP)[None, None, :, :, None, None],
                            s.shape[:-1] + (1,))
    p = jax.nn.softmax(jnp.concatenate([s, sink], axis=-1), axis=-1)[..., :-1]
    return jnp.einsum('nckgij,ncjkd->ncikgd', p, v.astype(f32))


def swa_prompt(q, k, v, sinks):
    n, L = q.shape[:2]
    nc = L // CHUNK
    pad = WIN_CHUNKS * CHUNK

    def band(t):
        tp = jnp.pad(t, ((0, 0), (pad, 0), (0, 0), (0, 0)))
        tc = tp.reshape(n, nc + WIN_CHUNKS, CHUNK, SWA_KV_HEADS, SWA_HEAD_DIM)
        return jnp.concatenate([tc[:, w:w + nc] for w in range(WIN_CHUNKS + 1)], axis=2)

    qb = q.reshape(n, nc, CHUNK, SWA_KV_HEADS, SWA_GROUP, SWA_HEAD_DIM)
    qpos = jnp.arange(L).reshape(nc, CHUNK)
    kpos = jnp.arange(nc)[:, None] * CHUNK - pad + jnp.arange((WIN_CHUNKS + 1) * CHUNK)[None, :]
    o = banded_attention(qb, band(k), band(v), qpos, kpos, sinks)
    return o.reshape(n, L, SWA_HEADS * SWA_HEAD_DIM)


def swa_sample(q, k, v, k_cache, v_cache, sinks):
    n, L = q.shape[:2]
    wc = k_cache.shape[1]
    k_all = jnp.concatenate([k_cache.astype(k.dtype), k], axis=1)[:, None]
    v_all = jnp.concatenate([v_cache.astype(v.dtype), v], axis=1)[:, None]
    qpos = (PAST_LEN + jnp.arange(L))[None]
    kpos = jnp.concatenate([PAST_LEN - wc + jnp.arange(wc), PAST_LEN + jnp.arange(L)])[None]
    o = banded_attention(q[:, None], k_all, v_all, qpos, kpos, sinks)
    return o.reshape(n, L, SWA_HEADS * SWA_HEAD_DIM)


def mixer_shortconv(u, buf, w):
    bg, cg, h = jnp.split(u, 3, axis=-1)
    y, new_buf = causal_dwconv(cg * h, buf, w)
    return bg * y, new_buf


def mixer_conformer(u, buf, w, b, ln_g, ln_b):
    a, g = jnp.split(u, 2, axis=-1)
    glu = a * jax.nn.sigmoid(g)
    y, new_buf = causal_dwconv(glu, buf, w, b)
    y = layer_norm(y, ln_g, ln_b)
    return jax.nn.silu(y), new_buf


def trunk_layer(x, h0, ssm_buf, k_cache, v_cache, sc_buf, cc_buf,
                norm_mix, w_in, ssm_conv_w, ssm_conv_b, ssm_dt_bias, ssm_a_log, ssm_d, ssm_norm,
                swa_sinks, sconv_w, cconv_w, cconv_b, cconv_ln_g, cconv_ln_b, w_out,
                norm_ffn, w_gate, w_up, w_down):
    n, L, _ = x.shape
    prompt = k_cache is None
    u = rms_norm(x, norm_mix) @ w_in
    z, xbc, dt_raw, q, k, v, u_sc, u_cc = jnp.split(u, IN_SPLITS, axis=-1)
    ya, ssm_buf_new, h_new = mixer_ssd(z, xbc, dt_raw, ssm_buf, h0, ssm_conv_w, ssm_conv_b, ssm_dt_bias,
                                       ssm_a_log, ssm_d, ssm_norm, CHUNK if prompt else L)
    q = q.reshape(n, L, SWA_KV_HEADS, SWA_GROUP, SWA_HEAD_DIM)
    k = k.reshape(n, L, SWA_KV_HEADS, SWA_HEAD_DIM)
    v = v.reshape(n, L, SWA_KV_HEADS, SWA_HEAD_DIM)
    if prompt:
        yb = swa_prompt(q, k, v, swa_sinks)
        keep = min(WINDOW, L)
        k_new, v_new = k[:, L - keep:], v[:, L - keep:]
    else:
        yb = swa_sample(q, k, v, k_cache, v_cache, swa_sinks)
        k_new, v_new = k, v
    yc, sc_new = mixer_shortconv(u_sc, sc_buf, sconv_w)
    yd, cc_new = mixer_conformer(u_cc, cc_buf, cconv_w, cconv_b, cconv_ln_g, cconv_ln_b)
    mix = jnp.concatenate([ya, yb.astype(x.dtype), yc, yd], axis=-1)
    x = x + mix @ w_out
    hf = rms_norm(x, norm_ffn)
    x = x + (jax.nn.silu(hf @ w_gate) * (hf @ w_up)) @ w_down
    return x, h_new, ssm_buf_new, k_new, v_new, sc_new, cc_new


def setup_inputs(seed: int = 0) -> dict:
    key = jax.random.key(seed)
    ks = jax.random.split(key, 32)
    f32 = jnp.float32

    def nrm(k, shape, s):
        return s * jax.random.normal(k, shape, f32)

    win_cache = min(WINDOW, PAST_LEN)
    dt0 = jnp.exp(jax.random.uniform(ks[10], (DEPTH, SSM_HEADS), f32, math.log(1e-3), math.log(1e-1)))
    return {
        'x_prompt': nrm(ks[0], (BATCH, SEQ, D_MODEL), 1.0),
        'x_sample': nrm(ks[1], (DEC_BATCH, DEC_SEQ, D_MODEL), 1.0),
        'state_ssm': nrm(ks[2], (DEPTH, DEC_BATCH, SSM_HEADS, SSM_HEAD_DIM, SSM_STATE), 0.5),
        'state_ssm_conv': nrm(ks[3], (DEPTH, DEC_BATCH, SSM_CONV - 1, SSM_CONV_DIM), 1.0),
        'cache_swa_k': nrm(ks[4], (DEPTH, DEC_BATCH, win_cache, SWA_KV_HEADS, SWA_HEAD_DIM), 1.0),
        'cache_swa_v': nrm(ks[5], (DEPTH, DEC_BATCH, win_cache, SWA_KV_HEADS, SWA_HEAD_DIM), 1.0),
        'state_sconv': nrm(ks[6], (DEPTH, DEC_BATCH, SC_CONV - 1, SC_WIDTH), 1.0),
        'state_cconv': nrm(ks[7], (DEPTH, DEC_BATCH, CC_CONV - 1, CC_WIDTH), 0.5),
        'norm_mix': 1.0 + nrm(ks[8], (DEPTH, D_MODEL), 0.05),
        'w_in': nrm(ks[9], (DEPTH, D_MODEL, D_IN), D_MODEL ** -0.5),
        'ssm_conv_w': nrm(ks[11], (DEPTH, SSM_CONV, SSM_CONV_DIM), SSM_CONV ** -0.5),
        'ssm_conv_b': nrm(ks[12], (DEPTH, SSM_CONV_DIM), 0.02),
        'ssm_dt_bias': dt0 + jnp.log(-jnp.expm1(-dt0)),
        'ssm_a_log': jnp.log(jax.random.uniform(ks[13], (DEPTH, SSM_HEADS), f32, 1.0, 16.0)),
        'ssm_d': 1.0 + nrm(ks[14], (DEPTH, SSM_HEADS), 0.1),
        'ssm_norm': 1.0 + nrm(ks[15], (DEPTH, SSM_INNER), 0.05),
        'swa_sinks': nrm(ks[16], (DEPTH, SWA_HEADS), 0.5),
        'sconv_w': nrm(ks[17], (DEPTH, SC_CONV, SC_WIDTH), SC_CONV ** -0.5),
        'cconv_w': nrm(ks[18], (DEPTH, CC_CONV, CC_WIDTH), CC_CONV ** -0.5),
        'cconv_b': nrm(ks[19], (DEPTH, CC_WIDTH), 0.02),
        'cconv_ln_g': 1.0 + nrm(ks[20], (DEPTH, CC_WIDTH), 0.05),
        'cconv_ln_b': nrm(ks[21], (DEPTH, CC_WIDTH), 0.02),
        'w_out': nrm(ks[22], (DEPTH, D_MIX, D_MODEL), D_MIX ** -0.5),
        'norm_ffn': 1.0 + nrm(ks[23], (DEPTH, D_MODEL), 0.05),
        'w_gate': nrm(ks[24], (DEPTH, D_MODEL, D_FF), D_MODEL ** -0.5),
        'w_up': nrm(ks[25], (DEPTH, D_MODEL, D_FF), D_MODEL ** -0.5),
        'w_down': nrm(ks[26], (DEPTH, D_FF, D_MODEL), D_FF ** -0.5),
        'norm_final': 1.0 + nrm(ks[27], (D_MODEL,), 0.05),
    }


def reference(x_prompt, x_sample, state_ssm, state_ssm_conv, cache_swa_k, cache_swa_v, state_sconv, state_cconv,
              norm_mix, w_in, ssm_conv_w, ssm_conv_b, ssm_dt_bias, ssm_a_log, ssm_d, ssm_norm, swa_sinks,
              sconv_w, cconv_w, cconv_b, cconv_ln_g, cconv_ln_b, w_out, norm_ffn, w_gate, w_up, w_down,
              norm_final):
    act_dtype = x_prompt.dtype
    xp, xs = x_prompt, x_sample
    nb = xp.shape[0]
    new_p = [[] for _ in range(6)]
    new_s = [[] for _ in range(6)]
    for l in range(DEPTH):
        lw = (norm_mix[l], w_in[l], ssm_conv_w[l], ssm_conv_b[l], ssm_dt_bias[l], ssm_a_log[l], ssm_d[l],
              ssm_norm[l], swa_sinks[l], sconv_w[l], cconv_w[l], cconv_b[l], cconv_ln_g[l], cconv_ln_b[l],
              w_out[l], norm_ffn[l], w_gate[l], w_up[l], w_down[l])
        xp, *sp = trunk_layer(xp,
                              jnp.zeros((nb, SSM_HEADS, SSM_HEAD_DIM, SSM_STATE), act_dtype),
                              jnp.zeros((nb, SSM_CONV - 1, SSM_CONV_DIM), act_dtype),
                              None, None,
                              jnp.zeros((nb, SC_CONV - 1, SC_WIDTH), act_dtype),
                              jnp.zeros((nb, CC_CONV - 1, CC_WIDTH), act_dtype),
                              *lw)
        xs, *ss = trunk_layer(xs, state_ssm[l], state_ssm_conv[l], cache_swa_k[l], cache_swa_v[l],
                              state_sconv[l], state_cconv[l], *lw)
        for i in range(6):
            new_p[i].append(sp[i])
            new_s[i].append(ss[i])
    ssm_p, ssm_conv_p, swa_k_p, swa_v_p, sconv_p, cconv_p = [jnp.stack(a, axis=0) for a in new_p]
    ssm_s, ssm_conv_s, swa_k_s, swa_v_s, sconv_s, cconv_s = [jnp.stack(a, axis=0) for a in new_s]
    y_prompt = rms_norm(xp, norm_final)
    y_sample = rms_norm(xs, norm_final)
    return (y_prompt, y_sample, ssm_p, ssm_s, ssm_conv_p, ssm_conv_s, swa_k_p, swa_k_s, swa_v_p, swa_v_s,
            sconv_p, sconv_s, cconv_p, cconv_s)
```

```python
import numpy as np
from contextlib import ExitStack
import concourse.bass as bass
import concourse.mybir as mybir
from concourse.bass_utils import run_bass_kernel_spmd

F32 = mybir.dt.float32
BF16 = mybir.dt.bfloat16
ALU = mybir.AluOpType
AF = mybir.ActivationFunctionType
AX = mybir.AxisListType

COMPUTE = ("pe", "act", "dve", "pool")
QUEUES = ("sp", "pool")
DMA_POOL = 24
EPS = 1e-5
D = 1024
DIN = 2564
DFF = 2816
NFF = 22
NEG = -30000.0


class Buf:
    __slots__ = ("name", "lw", "rd_c", "rd_d", "excl")

    def __init__(self, name, excl=False):
        self.name = name
        self.lw = None
        self.rd_c = {}
        self.rd_d = []
        self.excl = excl


class Op:
    __slots__ = ("eng", "fn", "deps", "dma", "idx", "signal", "cnt", "sem", "target", "waits", "gidx")

    def __init__(self, eng, fn, dma):
        self.eng = eng
        self.fn = fn
        self.dma = dma
        self.deps = []
        self.signal = False
        self.cnt = 0
        self.sem = None
        self.target = 0
        self.waits = []


class Prog:
    def __init__(self, nc):
        self.nc = nc
        self.ops = {e: [] for e in ("pe", "act", "dve", "pool", "sp")}
        self.all = []
        self.fence_pending = {}

    def fence(self):
        f = []
        for e in COMPUTE:
            if self.ops[e]:
                f.append(self.ops[e][-1])
        f.extend(op for op in self.all if op.dma and op.gidx >= getattr(self, "_fence_g", 0))
        self._fence_g = len(self.all)
        self.fence_pending = {e: f for e in self.ops}

    def add(self, eng, fn, reads=(), writes=(), dma=False):
        op = Op(eng, fn, dma)
        op.idx = len(self.ops[eng])
        op.gidx = len(self.all)
        deps = {}

        def add_dep(d):
            if d is None or d is op:
                return
            if d.dma:
                deps[("dma", d.gidx)] = d
            else:
                k = ("c", d.eng)
                if k not in deps or deps[k].idx < d.idx:
                    deps[k] = d

        if eng in self.fence_pending:
            for d in self.fence_pending.pop(eng):
                add_dep(d)
        for b in reads:
            add_dep(b.lw)
            if b.excl:
                for e2, d in b.rd_c.items():
                    if e2 != eng:
                        add_dep(d)
        for b in writes:
            if b.lw is not None and (dma or b.lw.dma or b.lw.eng != eng):
                add_dep(b.lw)
            for e2, d in b.rd_c.items():
                if dma or e2 != eng:
                    add_dep(d)
            for d in b.rd_d:
                add_dep(d)
        for b in reads:
            if dma:
                b.rd_d.append(op)
            else:
                b.rd_c[eng] = op
        for b in writes:
            b.lw = op
            b.rd_c = {}
            b.rd_d = []
        op.deps = list(deps.values())
        self.ops[eng].append(op)
        self.all.append(op)
        return op

    @staticmethod
    def _skip(d, op):
        return d.eng == op.eng and op.eng == "pe" and not op.dma and not d.dma

    def finalize(self, sem_ctx):
        esem = {e: sem_ctx(f"c_{e}") for e in COMPUTE}
        dsem = {q: [sem_ctx(f"d_{q}{i}") for i in range(DMA_POOL)] for q in QUEUES}
        dcount = {q: [0] * DMA_POOL for q in QUEUES}
        dk = {q: 0 for q in QUEUES}
        for op in self.all:
            for d in op.deps:
                if d.dma or self._skip(d, op):
                    continue
                d.signal = True
        cnt = {e: 0 for e in COMPUTE}
        for op in self.all:
            if op.dma:
                q = op.eng
                k = dk[q] % DMA_POOL
                dk[q] += 1
                dcount[q][k] += 1
                op.sem = dsem[q][k]
                op.target = 16 * dcount[q][k]
                op.cnt = (q, k)
            elif op.signal:
                cnt[op.eng] += 1
                op.cnt = cnt[op.eng]
        for e, lst in self.ops.items():
            w = {}
            for op in lst:
                ws = []

                def need(sem, key, val):
                    if w.get(key, 0) >= val:
                        return
                    w[key] = val
                    ws.append((sem, val))

                if op.dma and op.target > 16:
                    need(op.sem, ("d",) + op.cnt, op.target - 16)
                for d in op.deps:
                    if d.dma:
                        need(d.sem, ("d",) + d.cnt, d.target)
                    elif not self._skip(d, op):
                        need(esem[d.eng], ("c", d.eng), d.cnt)
                op.waits = ws
        self._esem = esem
        self._final = [(dsem[q][k], 16 * dcount[q][k]) for q in QUEUES for k in range(DMA_POOL) if dcount[q][k] > 0]
        self.sig_counts = cnt

    def run_engine(self, e, eng, last=False):
        esem = self._esem
        for op in self.ops[e]:
            for sem, val in op.waits:
                eng.wait_ge(sem, val)
            ins = op.fn(eng)
            if op.dma:
                ins.then_inc(op.sem, 16)
            elif op.signal:
                ins.then_inc(esem[e], 1)
        if last:
            for sem, val in self._final:
                eng.wait_ge(sem, val)


class TT:
    def __init__(self, t, buf):
        self.t = t
        self.b = buf

    def __getitem__(self, k):
        return self.t[k]


def _consts():
    c = {}
    c["ident"] = np.eye(128, dtype=np.float32)
    k = np.arange(128)
    c["utri"] = (k[:, None] <= k[None, :]).astype(np.float32)
    mb = np.where(k[None, :] >= k[:, None], 0.0, NEG).astype(np.float32)
    c["maskb4"] = np.ascontiguousarray(np.broadcast_to(mb[:, None, :], (128, 4, 128))).astype(np.float32)
    slopes = (2.0 ** (-8.0 * np.arange(1, 5, dtype=np.float32) / 4.0)).astype(np.float32)
    j = np.arange(128)[:, None, None].astype(np.float32)
    i = np.arange(128)[None, None, :].astype(np.float32)
    sl = slopes[None, :, None]
    bp = -sl * np.abs(i - (j - 128.0))
    bp = np.where((j < 64) & (i >= 64), NEG, bp)
    c["biasP"] = np.ascontiguousarray(bp).astype(np.float32)
    bc = -sl * np.abs(i - j)
    bc = np.where((j >= 64) & (i < 64), NEG, bc)
    c["biasC"] = np.ascontiguousarray(bc).astype(np.float32)
    i16 = np.arange(16)[None, None, :].astype(np.float32)
    c["biasSc"] = np.ascontiguousarray(-sl * np.abs((1024.0 + i16) - (896.0 + j))).astype(np.float32)
    j16 = np.arange(16)[:, None, None].astype(np.float32)
    c["biasSn"] = np.ascontiguousarray(-sl * np.abs(i16 - j16)).astype(np.float32)
    return c


NPF = 94
NPR = 272


def _pack_params(ssm_conv_w, ssm_conv_b, ssm_dt_bias, ssm_a_log, ssm_d, ssm_norm, swa_sinks, sconv_w, cconv_w,
                 cconv_b, cconv_ln_g, cconv_ln_b):
    Ld = ssm_conv_w.shape[0]
    pfm = np.zeros((Ld, 128, NPF), np.float32)
    prow = np.zeros((Ld, NPR), np.float32)
    for l in range(Ld):
        o = 0
        pfm[l, :, o:o + 16] = ssm_conv_w[l].reshape(4, 4, 128).transpose(2, 1, 0).reshape(128, 16); o += 16
        pfm[l, :, o:o + 4] = ssm_conv_b[l].reshape(4, 128).T; o += 4
        pfm[l, :, o:o + 6] = sconv_w[l].reshape(3, 2, 128).transpose(2, 1, 0).reshape(128, 6); o += 6
        pfm[l, :, o:o + 62] = cconv_w[l].reshape(31, 2, 128).transpose(2, 1, 0).reshape(128, 62); o += 62
        pfm[l, :, o:o + 2] = cconv_b[l].reshape(2, 128).T; o += 2
        pfm[l, :, o:o + 2] = cconv_ln_g[l].reshape(2, 128).T; o += 2
        pfm[l, :, o:o + 2] = cconv_ln_b[l].reshape(2, 128).T; o += 2
        prow[l, 0:4] = ssm_dt_bias[l]
        prow[l, 4:8] = ssm_a_log[l]
        prow[l, 8:12] = ssm_d[l]
        prow[l, 12:16] = swa_sinks[l]
        prow[l, 16:272] = ssm_norm[l]
    return pfm, prow


def build(NPT, DEPTH=2):
    nc = bass.Bass("TRN2", target_bir_lowering=False)
    es = ExitStack()
    P = Prog(nc)
    NTP = NPT * 128

    def din(name, shape):
        return nc.dram_tensor(name, list(shape), F32, kind="ExternalInput").ap()

    def dout(name, shape):
        return nc.dram_tensor(name, list(shape), F32, kind="ExternalOutput").ap()

    xp = din("xp", [NTP, D]); xs = din("xs", [64, D])
    st_ssm = din("st_ssm", [DEPTH, 4, 256, 128]); st_xc = din("st_xc", [DEPTH, 4, 3, 512])
    ck = din("ck", [DEPTH, 4, 128, 128]); cv = din("cv", [DEPTH, 4, 128, 128])
    st_sc = din("st_sc", [DEPTH, 4, 2, 256]); st_cc = din("st_cc", [DEPTH, 4, 30, 256])
    w_in = din("w_in", [DEPTH, D, DIN]); w_out = din("w_out", [DEPTH, D, D])
    w_gate = din("w_gate", [DEPTH, D, DFF]); w_up = din("w_up", [DEPTH, D, DFF]); w_down = din("w_down", [DEPTH, DFF, D])
    norm_mix = din("norm_mix", [DEPTH, D]); norm_ffn = din("norm_ffn", [DEPTH, D]); norm_final = din("norm_final", [1, D])
    pfm_d = din("pfm", [DEPTH, 128, NPF]); prow_d = din("prow", [DEPTH, NPR]); ssm_conv_b_d = din("ssm_conv_b", [DEPTH, 512])
    c_ident = din("ident", [128, 128]); c_utri = din("utri", [128, 128]); c_maskb4 = din("maskb4", [128, 4, 128])
    c_biasP = din("biasP", [128, 4, 128]); c_biasC = din("biasC", [128, 4, 128])
    c_biasSc = din("biasSc", [128, 4, 16]); c_biasSn = din("biasSn", [16, 4, 16])

    yp = dout("yp", [NTP, D]); ys = dout("ys", [64, D])
    o_ssm_p = dout("o_ssm_p", [DEPTH, 256, 128]); o_ssm_s = dout("o_ssm_s", [DEPTH, 4, 256, 128])
    o_xc_p = dout("o_xc_p", [DEPTH, 3, 512]); o_xc_s = dout("o_xc_s", [DEPTH, 4, 3, 512])
    o_k_p = dout("o_k_p", [DEPTH, 128, 128]); o_k_s = dout("o_k_s", [DEPTH, 4, 16, 128])
    o_v_p = dout("o_v_p", [DEPTH, 128, 128]); o_v_s = dout("o_v_s", [DEPTH, 4, 16, 128])
    o_sc_p = dout("o_sc_p", [DEPTH, 2, 256]); o_sc_s = dout("o_sc_s", [DEPTH, 4, 2, 256])
    o_cc_p = dout("o_cc_p", [DEPTH, 30, 256]); o_cc_s = dout("o_cc_s", [DEPTH, 4, 30, 256])

    xa_p = nc.dram_tensor("xa_p", [NTP, D], F32, kind="Internal").ap()
    xa_s = nc.dram_tensor("xa_s", [64, D], F32, kind="Internal").ap()
    xb_p = nc.dram_tensor("xb_p", [NTP, D], F32, kind="Internal").ap()
    xb_s = nc.dram_tensor("xb_s", [64, D], F32, kind="Internal").ap()
    dbuf = {}

    def dB(key):
        if key not in dbuf:
            dbuf[key] = Buf(str(key))
        return dbuf[key]

    _n = [0]

    def sb(shape, dt=F32, name=None):
        _n[0] += 1
        nm = f"{name or 't'}_{_n[0]}"
        return TT(es.enter_context(nc.sbuf_tensor(nm, list(shape), dt)), Buf(nm))

    class Arena:
        def __init__(self, base_ap, lo, hi):
            self.base = base_ap; self.lo = lo; self.hi = hi; self.cur = lo

        def reset(self):
            self.cur = self.lo

        def alloc(self, shape, dt, name):
            n = 1
            for d_ in shape[1:]:
                n *= d_
            nb16 = n * (2 if dt == F32 else 1)
            nb16 += nb16 % 2
            if self.cur + nb16 > self.hi:
                return None
            v = self.base[:, self.cur:self.cur + n * (2 if dt == F32 else 1)]
            self.cur += nb16
            if dt == F32:
                v = v.bitcast(F32)
            if len(shape) == 3:
                v = v.rearrange("p (a b) -> p a b", a=shape[1])
            if shape[0] < 128:
                v = v[0:shape[0]]
            return TT(v, Buf(name))

    arenas = {}

    def sbm(shape, dt=F32, name="m"):
        for a in arenas["m"]:
            r = a.alloc(shape, dt, name)
            if r is not None:
                return r
        return sb(shape, dt, name)

    def sbf(shape, dt=F32, name="f"):
        r = arenas["f"][0].alloc(shape, dt, name)
        assert r is not None, name
        return r

    banks = []
    for i in range(8):
        banks.append(TT(es.enter_context(nc.psum_tensor(f"bank{i}", [128, 512], F32)), Buf(f"bank{i}", excl=True)))

    def bf(bank):
        return bank.t[:].bitcast(BF16)

    def _bl(lst):
        out = []
        for x in lst:
            if isinstance(x, TT):
                out.append(x.b)
            elif isinstance(x, list):
                out.extend(x)
            else:
                out.append(x)
        return out

    def pe(fn, r, w):
        P.add("pe", fn, _bl(r), _bl(w))

    def act(fn, r, w):
        P.add("act", fn, _bl(r), _bl(w))

    def dve(fn, r, w):
        P.add("dve", fn, _bl(r), _bl(w))

    def pool(fn, r, w):
        P.add("pool", fn, _bl(r), _bl(w))

    def dma(q, out, in_, r, w):
        P.add(q, lambda e, out=out, in_=in_: e.dma_start(out=out, in_=in_), _bl(r), _bl(w), dma=True)

    WA = []
    WB = []

    def wdma(out, in_, lst=None):
        b = Buf("w")
        (WA if lst is None else lst).append(b)
        dma("pool", out, in_, [], [b])

    ident = sb([128, 128], F32, "ident"); identb = sb([128, 128], BF16, "identb")
    utri = sb([128, 128], F32, "utri"); onesf = sb([128, 128], F32, "onesf"); onesb = sb([128, 128], BF16, "onesb")
    ones256 = sb([128, 128], F32, "ones256")
    neghalf = sb([128, 128], F32, "neghalf")
    epsc = sb([128, 4], F32, "epsc")
    maskb4 = sb([128, 4, 128], F32, "maskb4")
    biasP = sb([128, 4, 128], F32, "biasP"); biasC = sb([128, 4, 128], F32, "biasC")
    biasSc = sb([128, 4, 16], F32, "biasSc"); biasSn = sb([16, 4, 16], F32, "biasSn")
    dma("sp", ident[:], c_ident, [], [ident]); dma("sp", utri[:], c_utri, [], [utri])
    dma("sp", maskb4[:], c_maskb4, [], [maskb4]); dma("sp", biasP[:], c_biasP, [], [biasP])
    dma("sp", biasC[:], c_biasC, [], [biasC]); dma("sp", biasSc[:], c_biasSc, [], [biasSc])
    dma("sp", biasSn[:], c_biasSn, [], [biasSn])
    dve(lambda e: e.tensor_copy(out=identb[:], in_=ident[:]), [ident], [identb])
    dve(lambda e: e.memset(onesf[:], 1.0), [], [onesf])
    dve(lambda e: e.memset(onesb[:], 1.0), [], [onesb])
    dve(lambda e: e.memset(ones256[:], 1.0 / 256.0), [], [ones256])
    dve(lambda e: e.memset(neghalf[:], -0.5), [], [neghalf])
    dve(lambda e: e.memset(epsc[:, 0:1], D * EPS), [], [epsc])
    dve(lambda e: e.memset(epsc[:, 1:2], 256.0 * EPS), [], [epsc])
    dve(lambda e: e.memset(epsc[:, 2:3], EPS), [], [epsc])

    gmix = sb([128, D], F32, "gmix"); gffn = sb([128, D], F32, "gffn"); gfin = sb([128, D], F32, "gfin")
    pfm = sb([128, NPF], F32, "pfm"); pfh = sb([128, NPF], F32, "pfh")
    prow = sb([128, NPR], F32, "prow")
    Aneg = sb([128, 4], F32, "Aneg"); esink = sb([128, 4], F32, "esink"); g16 = sb([128, 256], F32, "g16")
    dma("sp", gfin[:], norm_final.partition_broadcast(128), [], [gfin])
    dve(lambda e: e.tensor_scalar(out=gfin[:], in0=gfin[:], scalar1=32.0, scalar2=None, op0=ALU.mult), [gfin], [gfin])

    WM_IN = 8 * DIN
    warena = es.enter_context(nc.sbuf_tensor("warena", [128, 3 * 8 * DFF], BF16))
    w_in_sb = warena[:, 0:WM_IN].rearrange("p (k n) -> p k n", k=8)
    o1 = WM_IN
    w_out_m = warena[:, o1:o1 + 6 * D].rearrange("p (k n) -> p k n", k=6); o1 += 6 * D
    w_out_b = warena[0:64, o1:o1 + 4 * D].rearrange("p (k n) -> p k n", k=4); o1 += 4 * D
    diag = warena[:, o1:o1 + 62 * 128].rearrange("p (k n) -> p k n", k=62); o1 += 62 * 128
    sarena = es.enter_context(nc.sbuf_tensor("sarena", [128, 22528], BF16))
    arenas["m"] = [Arena(warena[:, :], o1, 3 * 8 * DFF), Arena(sarena[:, :], 0, 22528)]
    arenas["f"] = [Arena(sarena[:, :], 0, 22528)]
    wg_sb = warena[:, 0:8 * DFF].rearrange("p (k n) -> p k n", k=8)
    wu_sb = warena[:, 8 * DFF:16 * DFF].rearrange("p (k n) -> p k n", k=8)
    wd_sb = warena[:, 16 * DFF:24 * DFF].rearrange("p (k n) -> p k n", k=NFF)

    x_t = [sbm([128, D], F32, "x_t") for _ in range(2)]
    junk = sb([128, D], BF16, "junk")
    xn2 = [sbm([128, D], BF16, "xn") for _ in range(2)]
    ss = sb([128, 8], F32, "ss"); rs = sb([128, 8], F32, "rs")
    xnT2 = [sbm([128, 8, 128], BF16, "xnT") for _ in range(2)]
    cbx = sbm([128, 4, 131], F32, "cbx"); tnx = sbm([128, 4, 128], F32, "tnx")
    cbxb = sbm([128, 4, 132], BF16, "cbxb"); cbsb = sbm([128, 2, 130], BF16, "cbsb")
    dgx = sbm([128, 22, 128], BF16, "dgx")
    cbrow = sb([1, 512], BF16, "cbrow")
    xbcs = sbm([128, 4, 128], BF16, "xbcs")
    dtr = sbm([128, 4], F32, "dtr"); dtv = sbm([128, 4], F32, "dtv"); av = sbm([128, 4], F32, "av")
    eac = sbm([128, 4], F32, "eac"); bd = sbm([128, 4], F32, "bd")
    xs_tok = sbm([128, 256], BF16, "xs_tok"); B_tok = sbm([128, 128], BF16, "B_tok")
    xdt = sbm([128, 256], BF16, "xdt"); xdte = sbm([128, 256], BF16, "xdte")
    aU = sbm([128, 4, 128], F32, "aU"); na = sbm([128, 4, 128], F32, "na"); decay = sbm([128, 4, 128], F32, "decay")
    MT = sbm([128, 4, 128], BF16, "MT")
    y1 = sbm([128, 256], F32, "y1"); y3 = sbm([128, 256], F32, "y3"); tz = sbm([128, 256], F32, "tz"); g1 = sbm([128, 256], F32, "g1")
    ynb = sbm([128, 256], BF16, "ynb")
    mixa = sbm([128, 2, 128], BF16, "mixa"); mixb = sbm([64, 4, 128], BF16, "mixb")
    mixc = sbm([128, 2, 128], BF16, "mixc"); mixd = sbm([128, 2, 128], BF16, "mixd")
    qT = sbm([64, 4, 128], BF16, "qT")
    kv32 = sbm([128, 256], F32, "kv32")
    stmp = [sbm([128, 4, 128], F32, "stmp") for _ in range(2)]
    Eb = [sbm([128, 4, 128], BF16, "Eb") for _ in range(2)]
    rden = sbm([64, 4, 128], F32, "rden")
    h_sb = sbm([128, 2, 128], F32, "h_sb"); cbs = sbm([128, 2, 130], F32, "cbs")
    tg = sbm([128, 2, 128], F32, "tg"); glu32 = sbm([128, 2, 128], F32, "glu32")
    ycs = sbm([128, 2, 128], F32, "ycs"); ysq = sbm([128, 2, 128], F32, "ysq")
    lnv = sbm([128, 128], F32, "lnv"); lnr = sbm([128, 128], F32, "lnr"); lnd = sbm([128, 2, 128], F32, "lnd")
    tl = sbm([128, 2, 128], F32, "tl")
    hst = sbm([128, 256], F32, "hst")
    otrs = [sbm([128, 128], F32, "otr") for _ in range(4)]
    _rot = [0, 0]
    ldts = [sbm([128, 512], F32, "ldt") for _ in range(2)]

    class Ctx:
        pass

    def mkctx(nm):
        c = Ctx()
        c.hT = sbm([128, 256], F32, nm + "hT"); c.hTb = sbm([128, 256], BF16, nm + "hTb")
        c.tx = sbm([128, 4, 3], F32, nm + "tx"); c.tsc = sbm([128, 2, 2], F32, nm + "tsc")
        c.cbc = sbm([128, 2, 158], BF16, nm + "cbc")
        c.kT = [sbm([64, 2, 128], BF16, nm + "kT") for _ in range(2)]
        c.v = [sbm([128, 2, 64], BF16, nm + "v") for _ in range(2)]
        c.par = 0
        c.has_prev = False
        return c

    ctxP = mkctx("P")
    ctxS = mkctx("S")

    def transpose_to(dst_ap, dst_tt, src_ap, src_tt, R, C, bank, col0=0, eng="dve", dt32=True):
        idt = ident if dt32 else identb
        if dt32:
            pv = bank.t[0:C, col0:col0 + R]
        else:
            pv = bf(bank)[0:C, col0:col0 + R]
        pe(lambda e: e.transpose(pv, src_ap, idt[0:R, 0:R]), [src_tt, idt], [bank])
        if eng == "act":
            act(lambda e: e.activation(out=dst_ap, in_=pv, func=AF.Copy), [bank], [dst_tt])
        else:
            dve(lambda e: e.tensor_copy(out=dst_ap, in_=pv), [bank], [dst_tt])

    def rsqrt_col(dst_ap, src_ap, tts, n_mult, add_c, width=1):
        ecol = 0 if add_c == D * EPS else 1
        P_ = dst_ap.shape[0]
        act(lambda e: e.activation(out=dst_ap, in_=src_ap, func=AF.Ln, bias=epsc[0:P_, ecol:ecol + 1]), [tts[1], epsc], [tts[0]])
        act(lambda e: e.activation(out=dst_ap, in_=dst_ap, func=AF.Exp, scale=-0.5), [tts[0]], [tts[0]])

    def sigm(out_ap, out_tt, in_ap, in_tt, P_):
        act(lambda e: e.activation(out=out_ap, in_=in_ap, func=AF.Exp, scale=-1.0), [in_tt], [out_tt])
        act(lambda e: e.activation(out=out_ap, in_=out_ap, func=AF.Ln, bias=onesf[0:P_, 0:1]), [out_tt, onesf], [out_tt])
        act(lambda e: e.activation(out=out_ap, in_=out_ap, func=AF.Exp, scale=-1.0), [out_tt], [out_tt])

    def rmsnorm_tile(x_tt, x_ap, L, g_tt, out_ap, out_tt, col):
        dve(lambda e: e.memset(ss[0:L, col:col + 1], 0.0), [], [ss])
        act(lambda e: e.activation(out=junk[0:L, :], in_=x_ap, func=AF.Square, accum_out=ss[0:L, col:col + 1]),
            [x_tt, ss], [junk, ss])
        rsqrt_col(rs[0:L, col:col + 1], ss[0:L, col:col + 1], (rs, ss), 1.0, D * EPS)
        dve(lambda e: e.scalar_tensor_tensor(out=out_ap, in0=x_ap, scalar=rs[0:L, col:col + 1], in1=g_tt[0:L, :],
                                             op0=ALU.mult, op1=ALU.mult), [x_tt, rs, g_tt], [out_tt])

    def load_layer_params(l):
        dma("sp", gmix[:], norm_mix[l:l + 1, :].partition_broadcast(128), [], [gmix])
        dma("sp", gffn[:], norm_ffn[l:l + 1, :].partition_broadcast(128), [], [gffn])
        dma("sp", pfm[:], pfm_d[l], [], [pfm])
        dma("sp", prow[:], prow_d[l:l + 1, :].partition_broadcast(128), [], [prow])
        dve(lambda e: e.tensor_scalar(out=gmix[:], in0=gmix[:], scalar1=32.0, scalar2=None, op0=ALU.mult), [gmix], [gmix])
        dve(lambda e: e.tensor_scalar(out=gffn[:], in0=gffn[:], scalar1=32.0, scalar2=None, op0=ALU.mult), [gffn], [gffn])
        dve(lambda e: e.tensor_scalar(out=pfh[:], in0=pfm[:], scalar1=0.5, scalar2=None, op0=ALU.mult), [pfm], [pfh])
        act(lambda e: e.activation(out=Aneg[:], in_=prow[:, 4:8], func=AF.Exp), [prow], [Aneg])
        dve(lambda e: e.tensor_scalar(out=Aneg[:], in0=Aneg[:], scalar1=-1.0, scalar2=None, op0=ALU.mult), [Aneg], [Aneg])
        act(lambda e: e.activation(out=esink[:], in_=prow[:, 12:16], func=AF.Exp), [prow], [esink])
        dve(lambda e: e.tensor_scalar(out=g16[:], in0=prow[:, 16:272], scalar1=16.0, scalar2=None, op0=ALU.mult), [prow], [g16])

    diagB = Buf("diag")

    WIN_GROUPS = [(256, 768), (768, 1284), (1284, 2052), (2052, 2564), (0, 256)]
    WG = {}

    def wbufs(c0, c1):
        out = []
        for gi, (a, b) in enumerate(WIN_GROUPS):
            if c0 < b and c1 > a:
                out.extend(WG[gi])
        return out

    FFN_GROUPS = [(0, 6), (6, 12), (12, 17), (17, 22)]
    WF = {}

    def load_mixer_weights(l):
        del WA[:]
        del WB[:]
        WB.append(diagB)
        w_in_v = w_in[l].rearrange("(k p) n -> p k n", p=128)
        for gi, (c0, c1) in enumerate(WIN_GROUPS):
            WG[gi] = []
            wdma(w_in_sb[:, :, c0:c1], w_in_v[:, :, c0:c1], WG[gi])
            WA.extend(WG[gi])
        for i, r0 in enumerate((0, 128, 512, 640, 768, 896)):
            wdma(w_out_m[:, i, :], w_out[l, r0:r0 + 128, :], WB)
        for h in range(4):
            wdma(w_out_b[:, h, :], w_out[l, 256 + 64 * h:320 + 64 * h, :], WB)
        for i_ in range(22):
            col = i_ if i_ < 16 else 20 + (i_ - 16)
            dve(lambda e, i_=i_, col=col: e.tensor_scalar(out=dgx[:, i_, :], in0=ident[:], scalar1=pfm[:, col:col + 1], scalar2=None, op0=ALU.mult),
                [ident, pfm], [dgx])
        dma("pool", cbrow[:], ssm_conv_b_d[l:l + 1, :], [], [cbrow])
        for c in range(2):
            for k in range(31):
                col = 26 + c * 31 + k
                dve(lambda e, c=c, k=k, col=col: e.tensor_scalar(out=diag[:, c * 31 + k, :], in0=ident[:], scalar1=pfm[:, col:col + 1],
                                                                 scalar2=None, op0=ALU.mult), [ident, pfm], [diagB])

    def load_ffn_weights(l):
        del WA[:]
        del WB[:]
        wg_v = w_gate[l].rearrange("(k p) n -> p k n", p=128)
        wu_v = w_up[l].rearrange("(k p) n -> p k n", p=128)
        for gi, (j0, j1) in enumerate(FFN_GROUPS):
            WF[gi] = []
            wdma(wg_sb[:, :, j0 * 128:j1 * 128], wg_v[:, :, j0 * 128:j1 * 128], WF[gi])
            wdma(wu_sb[:, :, j0 * 128:j1 * 128], wu_v[:, :, j0 * 128:j1 * 128], WF[gi])
            WA.extend(WF[gi])
        for j in range(NFF):
            wdma(wd_sb[:, j, :], w_down[l, j * 128:(j + 1) * 128, :], WB)

    def init_prompt_ctx():
        c = ctxP
        dve(lambda e: e.memset(c.hT[:], 0.0), [], [c.hT])
        dve(lambda e: e.memset(c.hTb[:], 0.0), [], [c.hTb])
        dve(lambda e: e.memset(c.tx[:], 0.0), [], [c.tx])
        dve(lambda e: e.memset(c.tsc[:], 0.0), [], [c.tsc])
        dve(lambda e: e.memset(c.cbc[:], 0.0), [], [c.cbc])
        c.has_prev = False
        c.par = 0

    def init_sample_ctx(l, s):
        c = ctxS
        c.par = 0
        c.has_prev = True
        for hh in range(2):
            ldt = ldts[_rot[1] % 2]; _rot[1] += 1
            dma("pool", ldt[:, 0:128], st_ssm[l, s, hh * 128:(hh + 1) * 128, :], [], [ldt])
            transpose_to(c.hT[:, hh * 128:(hh + 1) * 128], c.hT, ldt[:, 0:128], ldt, 128, 128, banks[0])
        act(lambda e: e.activation(out=c.hTb[:], in_=c.hT[:], func=AF.Copy), [c.hT], [c.hTb])
        ldt = ldts[_rot[1] % 2]; _rot[1] += 1
        dma("pool", ldt[0:3, 0:512], st_xc[l, s], [], [ldt])
        for cc in range(4):
            transpose_to(c.tx[:, cc, :], c.tx, ldt[0:3, cc * 128:(cc + 1) * 128], ldt, 3, 128, banks[0])
        ldt = ldts[_rot[1] % 2]; _rot[1] += 1
        dma("pool", ldt[0:2, 0:256], st_sc[l, s], [], [ldt])
        for cc in range(2):
            transpose_to(c.tsc[:, cc, :], c.tsc, ldt[0:2, cc * 128:(cc + 1) * 128], ldt, 2, 128, banks[0])
        ldt = ldts[_rot[1] % 2]; _rot[1] += 1
        dma("pool", ldt[0:30, 0:256], st_cc[l, s], [], [ldt])
        for cc in range(2):
            transpose_to(c.cbc[:, cc, 0:30], c.cbc, ldt[0:30, cc * 128:(cc + 1) * 128], ldt, 30, 128, banks[0])
        ldt = ldts[_rot[1] % 2]; _rot[1] += 1
        dma("pool", ldt[:, 0:128], ck[l, s], [], [ldt])
        for g in range(2):
            transpose_to(c.kT[1][:, g, :], c.kT[1], ldt[:, g * 64:(g + 1) * 64], ldt, 128, 64, banks[0])
        ldt = ldts[_rot[1] % 2]; _rot[1] += 1
        dma("pool", ldt[:, 0:128], cv[l, s], [], [ldt])
        act(lambda e: e.activation(out=c.v[1][:].rearrange("p a b -> p (a b)"), in_=ldt[:, 0:128], func=AF.Copy), [ldt], [c.v[1]])

    import os

    def out_rows_T(dst_dram, src_ap, src_tt, R, C):
        pv = banks[0].t[0:R, 0:C]
        KORT = int(os.environ.get("KORT", "0"))
        if KORT != 2:
            pe(lambda e: e.transpose(pv, src_ap, ident[0:C, 0:C]), [src_tt, ident], [banks[0]])
            otr = otrs[_rot[0] % 4]; _rot[0] += 1
            dve(lambda e: e.tensor_copy(out=otr[0:R, 0:C], in_=pv), [banks[0]], [otr])
        if KORT != 1:
            dma("sp", dst_dram, otr[0:R, 0:C], [otr], [])

    def mixer_head_norm(L, x_src, x_src_b, ti):
        xt = x_t[ti % 2]; xn = xn2[ti % 2]
        dma("pool", xt[0:L, :], x_src, [x_src_b] if x_src_b is not None else [], [xt])
        rmsnorm_tile(xt, xt[0:L, :], L, gmix, xn[0:L, :], xn, 0)

    def mixer_head_tr(L, ti):
        xn = xn2[ti % 2]; xnT = xnT2[ti % 2]
        T0 = banks[0]
        T0v = bf(T0)
        def tr8(e):
            for k in range(8):
                ins = e.transpose(T0v[:, k * 128:k * 128 + L], xn[0:L, k * 128:(k + 1) * 128], identb[0:L, 0:L])
            return ins
        pe(tr8, [xn, identb], [T0])
        xnT_v = xnT[:, :, 0:L]
        act(lambda e: e.activation(out=xnT_v, in_=T0v.rearrange("p (k t) -> p k t", k=8)[:, :, 0:L], func=AF.Copy), [T0], [xnT])

    def mixer_front_A(c, L, ti):
        xnT = xnT2[ti % 2]
        bA, bK = banks[1], banks[3]
        def dt_group(e):
            for k in range(8):
                ins = e.matmul(bK.t[0:L, 408:412], lhsT=xnT[:, k, 0:L], rhs=w_in_sb[:, k, 768:772], start=(k == 0), stop=(k == 7))
            return ins
        pe(dt_group, [wbufs(768, 772), xnT], [bK])
        def f(e):
            for i, c0 in enumerate([256, 384, 512, 640]):
                for k in range(8):
                    ins = e.matmul(bA.t[0:128, i * L:(i + 1) * L], lhsT=w_in_sb[:, k, c0:c0 + 128], rhs=xnT[:, k, 0:L], start=(k == 0), stop=(k == 7))
            return ins
        pe(f, [wbufs(256, 768), xnT], [bA])
        dve(lambda e: e.tensor_copy(out=cbx[:, :, 0:3], in_=c.tx[:]), [c.tx], [cbx])
        act(lambda e: e.activation(out=cbx[:, :, 3:3 + L], in_=bA.t[:, 0:4 * L].rearrange("p (c t) -> p c t", c=4), func=AF.Copy),
            [bA], [cbx])
        act(lambda e: e.activation(out=cbxb[:, :, 0:3 + L], in_=cbx[:, :, 0:3 + L], func=AF.Copy), [cbx], [cbxb])
        dve(lambda e: e.tensor_copy(out=c.tx[:], in_=cbx[:, :, L:L + 3]), [cbx], [c.tx])
        dve(lambda e: e.tensor_tensor(out=dtr[0:L, :], in0=bK.t[0:L, 408:412], in1=prow[0:L, 0:4], op=ALU.add), [bK, prow], [dtr])

    def mixer_tile(l, c, L, x_src, x_src_b, x_dst, x_dst_b, kind, s, is_last, ti, pre_chain=None, pre_in=None, skip_A=False, next_A=None):
        if pre_in is not None:
            pre_in()
        xt = x_t[ti % 2]; xnT = xnT2[ti % 2]
        T0 = banks[0]
        T0v = bf(T0)
        cutt(3, ti)
        def fm_group(bank, ncol, M, cols):
            def f(e):
                for i, c0 in enumerate(cols):
                    for k in range(8):
                        ins = e.matmul(bank.t[0:M, i * L:(i + 1) * L], lhsT=w_in_sb[:, k, c0:c0 + M], rhs=xnT[:, k, 0:L],
                                       start=(k == 0), stop=(k == 7))
                return ins
            pe(f, [wbufs(min(cols), max(cols) + M), xnT], [bank])
        bA, bQ, bK, bS1, bS2, bC, bZ = banks[1], banks[2], banks[3], banks[4], banks[5], banks[6], banks[7]
        if not skip_A:
            mixer_front_A(c, L, ti)
        fm_group(bQ, 4, 64, [772, 836, 900, 964])
        fm_group(bK, 2, 64, [1028, 1092])
        fm_group(bS1, 4, 128, [1284, 1412, 1540, 1668])
        fm_group(bS2, 2, 128, [1796, 1924])
        fm_group(bC, 4, 128, [2052, 2180, 2308, 2436])
        def tok_group(e):
            for k in range(8):
                e.matmul(bZ.t[0:L, 0:256], lhsT=xnT[:, k, 0:L], rhs=w_in_sb[:, k, 0:256], start=(k == 0), stop=(k == 7))
            for k in range(8):
                ins = e.matmul(bZ.t[0:L, 256:512], lhsT=xnT[:, k, 0:L], rhs=w_in_sb[:, k, 1028:1284], start=(k == 0), stop=(k == 7))
            return ins
        pe(tok_group, [wbufs(0, 256), wbufs(1028, 1284), xnT], [bZ])
        cutt(4, ti)
        cutn(1, ti)
        cutn(2, ti)
        cutn(3, ti)
        cutn(4, ti)
        cutn(5, ti)
        act(lambda e: e.activation(out=qT[:, :, 0:L], in_=bQ.t[0:64, 0:4 * L].rearrange("p (c t) -> p c t", c=4), func=AF.Copy), [bQ], [qT])
        cutn(6, ti)
        kTc = c.kT[c.par]; vc = c.v[c.par]; kTp = c.kT[1 - c.par]; vp = c.v[1 - c.par]
        cutn(7, ti)
        act(lambda e: e.activation(out=kTc[:, :, 0:L], in_=bK.t[0:64, 0:2 * L].rearrange("p (c t) -> p c t", c=2), func=AF.Copy), [bK], [kTc])
        cutn(8, ti)
        dve(lambda e: e.tensor_copy(out=vc[0:L, :, :].rearrange("p a b -> p (a b)"), in_=bZ.t[0:L, 384:512]), [bZ], [vc])
        cutn(9, ti)
        if is_last or kind == "s":
            KVAR = int(os.environ.get("KVAR", "4"))
            if KVAR == 4:
                dve(lambda e: e.tensor_copy(out=kv32[0:L, :], in_=bZ.t[0:L, 256:512]), [bZ], [kv32])
            elif KVAR != 2:
                act(lambda e: e.activation(out=kv32[0:L, :], in_=bZ.t[0:L, 256:512], func=AF.Copy), [bZ], [kv32])
            if kind == "p" and KVAR != 1:
                dma("sp", o_k_p[l], kv32[:, 0:128], [kv32], [])
                if KVAR != 3:
                    dma("sp", o_v_p[l], kv32[:, 128:256], [kv32], [])
            elif kind == "p":
                pass
            else:
                dma("sp", o_k_s[l, s], kv32[0:L, 0:128], [kv32], [])
                dma("sp", o_v_s[l, s], kv32[0:L, 128:256], [kv32], [])
        cutn(10, ti)
        sigm(tz[0:L, :], tz, bZ.t[0:L, 0:256], bZ, L)
        cutn(11, ti)
        dve(lambda e: e.tensor_tensor(out=g1[0:L, :], in0=tz[0:L, :], in1=bZ.t[0:L, 0:256], op=ALU.mult), [tz, bZ], [g1])
        cutn(12, ti)
        act(lambda e: e.activation(out=bg_sb[:, :, 0:L], in_=bS1.t[:, 0:2 * L].rearrange("p (c t) -> p c t", c=2), func=AF.Copy), [bS1], [bg_sb])
        cutn(13, ti)
        act(lambda e: e.activation(out=h_sb[:, :, 0:L], in_=bS2.t[:, 0:2 * L].rearrange("p (c t) -> p c t", c=2), func=AF.Copy), [bS2], [h_sb])
        cutn(14, ti)
        dve(lambda e: e.tensor_copy(out=cbs[:, :, 0:2], in_=c.tsc[:]), [c.tsc], [cbs])
        cutn(15, ti)
        dve(lambda e: e.tensor_tensor(out=cbs[:, :, 2:2 + L], in0=bS1.t[:, 2 * L:4 * L].rearrange("p (c t) -> p c t", c=2),
                                      in1=h_sb[:, :, 0:L], op=ALU.mult), [bS1, h_sb], [cbs])
        act(lambda e: e.activation(out=cbsb[:, :, 0:2 + L], in_=cbs[:, :, 0:2 + L], func=AF.Copy), [cbs], [cbsb])
        cutn(16, ti)
        dve(lambda e: e.tensor_copy(out=c.tsc[:], in_=cbs[:, :, L:L + 2]), [cbs], [c.tsc])
        cutn(17, ti)
        sigm(tg[:, :, 0:L], tg, bC.t[:, 2 * L:4 * L].rearrange("p (c t) -> p c t", c=2), bC, 128)
        cutn(18, ti)
        dve(lambda e: e.tensor_tensor(out=glu32[:, :, 0:L], in0=tg[:, :, 0:L], in1=bC.t[:, 0:2 * L].rearrange("p (c t) -> p c t", c=2), op=ALU.mult),
            [tg, bC], [glu32])
        cutn(19, ti)
        cutn(20, ti)
        act(lambda e: e.activation(out=c.cbc[:, :, 30:30 + L], in_=glu32[:, :, 0:L], func=AF.Copy), [glu32], [c.cbc])

        cutt(5, ti)
        KOUT = int(os.environ.get("KOUT", "0"))
        if is_last or kind == "s":
            for cc in range(4 if KOUT in (0, 1) else 0):
                dst = (o_xc_p[l, :, cc * 128:(cc + 1) * 128] if kind == "p" else o_xc_s[l, s, :, cc * 128:(cc + 1) * 128])
                out_rows_T(dst, c.tx[:, cc, :], c.tx, 3, 128)
            for cc in range(2 if KOUT in (0, 2) else 0):
                dst = (o_sc_p[l, :, cc * 128:(cc + 1) * 128] if kind == "p" else o_sc_s[l, s, :, cc * 128:(cc + 1) * 128])
                out_rows_T(dst, c.tsc[:, cc, :], c.tsc, 2, 128)
            for cc in range(2 if KOUT in (0, 3) else 0):
                if kind == "p":
                    out_rows_T(o_cc_p[l, :, cc * 128:(cc + 1) * 128], glu32[:, cc, L - 30:L], glu32, 30, 128)
                else:
                    out_rows_T(o_cc_s[l, s, 14:30, cc * 128:(cc + 1) * 128], glu32[:, cc, 0:16], glu32, 16, 128)
            if kind == "s":
                dma("sp", o_cc_s[l, s, 0:14, :], st_cc[l, s, 16:30, :], [], [])

        cutt(6, ti)
        def gen_dt():
            act(lambda e: e.activation(out=dtv[0:L, :], in_=dtr[0:L, :], func=AF.Exp), [dtr], [dtv])
            yield
            act(lambda e: e.activation(out=dtv[0:L, :], in_=dtv[0:L, :], func=AF.Ln, bias=onesf[0:L, 0:1]), [dtv, onesf], [dtv])
            yield
            dve(lambda e: e.tensor_tensor(out=av[0:L, :], in0=dtv[0:L, :], in1=Aneg[0:L, :], op=ALU.mult), [dtv, Aneg], [av])
            yield
            dve(lambda e: e.tensor_tensor(out=aU[0:L, :, 0:L], in0=utri[0:L, 0:L].unsqueeze(1).to_broadcast([L, 4, L]),
                                          in1=av[0:L, :].unsqueeze(2).to_broadcast([L, 4, L]), op=ALU.mult), [utri, av], [aU])
            yield
            dve(lambda e: e.tensor_scalar(out=na[0:L, :, 0:L], in0=av[0:L, :].unsqueeze(2).to_broadcast([L, 4, L]), scalar1=-1.0, scalar2=None,
                                          op0=ALU.mult), [av], [na])
            yield
            def segmm(e):
                for h in range(4):
                    o = bA.t[0:L, h * L:(h + 1) * L]
                    e.matmul(o, lhsT=onesf[0:L, 0:L], rhs=aU[0:L, h, 0:L], start=True, stop=False)
                    e.matmul(o, lhsT=utri[0:L, 0:L], rhs=na[0:L, h, 0:L], start=False, stop=False)
                    ins = e.matmul(o, lhsT=ident[0:L, 0:L], rhs=maskb4[0:L, h, 0:L], start=False, stop=True)
                return ins
            pe(segmm, [onesf, utri, ident, maskb4, aU, na], [bA])
            yield
            act(lambda e: e.activation(out=decay[0:L, :, 0:L], in_=bA.t[0:L, 0:4 * L].rearrange("p (h t) -> p h t", h=4), func=AF.Exp), [bA], [decay])
            yield
            def smallmm(e):
                e.matmul(bK.t[0:L, 392:396], lhsT=utri[0:L, 0:L], rhs=av[0:L, :], start=True, stop=True)
                return e.matmul(bK.t[:, 400:404], lhsT=onesf[0:L, :], rhs=av[0:L, :], start=True, stop=True)
            pe(smallmm, [utri, onesf, av], [bK])
            yield
            act(lambda e: e.activation(out=eac[0:L, :], in_=bK.t[0:L, 392:396], func=AF.Exp), [bK], [eac])
            yield
            act(lambda e: e.activation(out=bd[:], in_=bK.t[:, 400:404], func=AF.Exp), [bK], [bd])
            yield

        def gen_state():
            dve(lambda e: e.tensor_tensor(out=xdte[0:L, :].rearrange("p (h d) -> p h d", h=4), in0=xdt[0:L, :].rearrange("p (h d) -> p h d", h=4),
                                          in1=decay[0:L, :, L - 1:L].to_broadcast([L, 4, 64]), op=ALU.mult), [xdt, decay], [xdte])
            yield
            pe(lambda e: e.matmul(bZ.t[:, 256:512], lhsT=B_tok[0:L, :], rhs=xdte[0:L, :], start=True, stop=True), [B_tok, xdte], [bZ])
            yield
            dve(lambda e: e.tensor_tensor(out=c.hT[:].rearrange("p (h d) -> p h d", h=4), in0=c.hT[:].rearrange("p (h d) -> p h d", h=4),
                                          in1=bd[:].unsqueeze(2).to_broadcast([128, 4, 64]), op=ALU.mult), [c.hT, bd], [c.hT])
            yield
            dve(lambda e: e.tensor_tensor(out=c.hT[:], in0=c.hT[:], in1=bZ.t[:, 256:512], op=ALU.add), [c.hT, bZ], [c.hT])
            yield
            act(lambda e: e.activation(out=c.hTb[:], in_=c.hT[:], func=AF.Copy), [c.hT], [c.hTb])
            yield
            if is_last or kind == "s":
                for hh in range(2):
                    dstd = (o_ssm_p[l, hh * 128:(hh + 1) * 128, :] if kind == "p" else o_ssm_s[l, s, hh * 128:(hh + 1) * 128, :])
                    out_rows_T(dstd, c.hT[:, hh * 128:(hh + 1) * 128], c.hT, 128, 128)
                    yield


        def gen_ssd():
            def xconv(e):
                for cc in range(4):
                    o = bS2.t[:, cc * L:(cc + 1) * L]
                    for k in range(4):
                        e.matmul(o, lhsT=dgx[:, cc * 4 + k, :], rhs=cbxb[:, cc, k:k + L], start=(k == 0), stop=False)
                    ins = e.matmul(o, lhsT=cbrow[0:1, cc * 128:(cc + 1) * 128], rhs=onesb[0:1, 0:L], start=False, stop=True)
                return ins
            pe(xconv, [dgx, cbxb, cbrow, onesb], [bS2])
            yield
            accv = bS2.t[:, 0:4 * L].rearrange("p (c t) -> p c t", c=4)
            act(lambda e: e.activation(out=tnx[:, :, 0:L], in_=accv, func=AF.Exp, scale=-1.0), [bS2], [tnx])
            yield
            act(lambda e: e.activation(out=tnx[:, :, 0:L], in_=tnx[:, :, 0:L], func=AF.Ln, bias=onesf[:, 0:1]), [tnx, onesf], [tnx])
            yield
            act(lambda e: e.activation(out=tnx[:, :, 0:L], in_=tnx[:, :, 0:L], func=AF.Exp, scale=-1.0), [tnx], [tnx])
            yield
            dve(lambda e: e.tensor_tensor(out=xbcs[:, :, 0:L], in0=tnx[:, :, 0:L], in1=accv, op=ALU.mult), [tnx, bS2], [xbcs])
            yield
            def trx(e):
                e.transpose(T0v[0:L, 0:128], xbcs[:, 0, 0:L], identb[:])
                e.transpose(T0v[0:L, 128:256], xbcs[:, 1, 0:L], identb[:])
                return e.transpose(T0v[0:L, 256:384], xbcs[:, 2, 0:L], identb[:])
            pe(trx, [xbcs, identb], [T0])
            yield
            act(lambda e: e.activation(out=xs_tok[0:L, :], in_=T0v[0:L, 0:256], func=AF.Copy), [T0], [xs_tok])
            yield
            act(lambda e: e.activation(out=B_tok[0:L, :], in_=T0v[0:L, 256:384], func=AF.Copy), [T0], [B_tok])
            yield
            dve(lambda e: e.tensor_tensor(out=xdt[0:L, :].rearrange("p (h d) -> p h d", h=4), in0=xs_tok[0:L, :].rearrange("p (h d) -> p h d", h=4),
                                          in1=dtv[0:L, :].unsqueeze(2).to_broadcast([L, 4, 64]), op=ALU.mult), [xs_tok, dtv], [xdt])
            yield
            dve(lambda e: e.tensor_tensor(out=y3[0:L, :].rearrange("p (h d) -> p h d", h=4), in0=xs_tok[0:L, :].rearrange("p (h d) -> p h d", h=4),
                                          in1=prow[0:L, 8:12].unsqueeze(2).to_broadcast([L, 4, 64]), op=ALU.mult), [xs_tok, prow], [y3])
            yield
            dve(lambda e: e.tensor_tensor(out=y3[0:L, :], in0=y3[0:L, :], in1=g1[0:L, :], op=ALU.mult), [y3, g1], [y3])
            yield
            pe(lambda e: e.matmul(bK.t[0:L, 0:L], lhsT=xbcs[:, 2, 0:L], rhs=xbcs[:, 3, 0:L], start=True, stop=True), [xbcs], [bK])
            yield
            dve(lambda e: e.tensor_tensor(out=MT[0:L, :, 0:L], in0=decay[0:L, :, 0:L], in1=bK.t[0:L, 0:L].unsqueeze(1).to_broadcast([L, 4, L]),
                                          op=ALU.mult), [decay, bK], [MT])
            yield
            def yintra(e):
                for h in range(4):
                    ins = e.matmul(bS1.t[0:L, h * 64:(h + 1) * 64], lhsT=MT[0:L, h, 0:L], rhs=xdt[0:L, h * 64:(h + 1) * 64], start=True, stop=True)
                return ins
            pe(yintra, [MT, xdt], [bS1])
            yield
            pe(lambda e: e.matmul(bS2.t[0:L, 0:256], lhsT=xbcs[:, 3, 0:L], rhs=c.hTb[:], start=True, stop=True), [xbcs, c.hTb], [bS2])
            yield
            gens.append(gen_state())
            dve(lambda e: e.tensor_tensor(out=y1[0:L, :].rearrange("p (h d) -> p h d", h=4), in0=bS2.t[0:L, 0:256].rearrange("p (h d) -> p h d", h=4),
                                          in1=eac[0:L, :].unsqueeze(2).to_broadcast([L, 4, 64]), op=ALU.mult), [bS2, eac], [y1])
            yield
            dve(lambda e: e.tensor_tensor(out=y1[0:L, :], in0=y1[0:L, :], in1=bS1.t[0:L, 0:256], op=ALU.add), [y1, bS1], [y1])
            yield
            dve(lambda e: e.tensor_tensor(out=y1[0:L, :], in0=y1[0:L, :], in1=g1[0:L, :], op=ALU.mult), [y1, g1], [y1])
            yield
            dve(lambda e: e.tensor_tensor(out=y1[0:L, :], in0=y1[0:L, :], in1=y3[0:L, :], op=ALU.add), [y1, y3], [y1])
            yield
            dve(lambda e: e.memset(ss[0:L, 1:2], 0.0), [], [ss])
            yield
            act(lambda e: e.activation(out=junk[0:L, 0:256], in_=y1[0:L, :], func=AF.Square, accum_out=ss[0:L, 1:2]), [y1, ss], [junk, ss])
            yield
            rsqrt_col(rs[0:L, 1:2], ss[0:L, 1:2], (rs, ss), 1.0, 256.0 * EPS)
            yield
            dve(lambda e: e.scalar_tensor_tensor(out=ynb[0:L, :], in0=y1[0:L, :], scalar=rs[0:L, 1:2], in1=g16[0:L, :], op0=ALU.mult, op1=ALU.mult),
                [y1, rs, g16], [ynb])
            yield
            def tra(e):
                e.transpose(T0v[:, 512:512 + L], ynb[0:L, 0:128], identb[0:L, 0:L])
                return e.transpose(T0v[:, 640:640 + L], ynb[0:L, 128:256], identb[0:L, 0:L])
            pe(tra, [ynb, identb], [T0])
            yield
            act(lambda e: e.activation(out=mixa[:, :, 0:L], in_=T0v[:, 512:768].rearrange("p (c t) -> p c t", c=2)[:, :, 0:L], func=AF.Copy), [T0], [mixa])
            yield

        def gen_cc():
            def sconvmm(e):
                for cc in range(2):
                    for k in range(3):
                        ins = e.matmul(bS1.t[:, 256 + cc * L:256 + (cc + 1) * L], lhsT=dgx[:, 16 + cc * 3 + k, :], rhs=cbsb[:, cc, k:k + L],
                                       start=(k == 0), stop=(k == 2))
                return ins
            pe(sconvmm, [dgx, cbsb], [bS1])
            yield
            dve(lambda e: e.tensor_tensor(out=mixc[:, :, 0:L], in0=bg_sb[:, :, 0:L],
                                          in1=bS1.t[:, 256:256 + 2 * L].rearrange("p (c t) -> p c t", c=2), op=ALU.mult), [bg_sb, bS1], [mixc])
            yield

            def ccmm(e):
                for cc in range(2):
                    for k in range(31):
                        ins = e.matmul(bZ.t[:, cc * L:(cc + 1) * L], lhsT=diag[:, cc * 31 + k, :], rhs=c.cbc[:, cc, k:k + L], start=(k == 0), stop=(k == 30))
                return ins
            pe(ccmm, [WB, c.cbc], [bZ])
            yield
            if kind == "p":
                dve(lambda e: e.tensor_copy(out=c.cbc[:, :, 0:30], in_=c.cbc[:, :, L:L + 30]), [c.cbc], [c.cbc])
                yield
            for cc in range(2):
                act(lambda e, cc=cc: e.activation(out=ycs[:, cc, 0:L], in_=bZ.t[:, cc * L:(cc + 1) * L], func=AF.Identity, bias=pfm[:, 88 + cc:89 + cc]),
                    [bZ, pfm], [ycs])
                yield
            act(lambda e: e.activation(out=ysq[:, :, 0:L], in_=ycs[:, :, 0:L], func=AF.Square), [ycs], [ysq])
            yield
            def lnmm(e):
                e.matmul(bK.t[:, 128:128 + L], lhsT=ones256[:], rhs=ycs[:, 0, 0:L], start=True, stop=False)
                e.matmul(bK.t[:, 128:128 + L], lhsT=ones256[:], rhs=ycs[:, 1, 0:L], start=False, stop=True)
                e.matmul(bK.t[:, 256:256 + L], lhsT=ones256[:], rhs=ysq[:, 0, 0:L], start=True, stop=False)
                return e.matmul(bK.t[:, 256:256 + L], lhsT=ones256[:], rhs=ysq[:, 1, 0:L], start=False, stop=True)
            pe(lnmm, [ones256, ycs, ysq], [bK])
            yield
            act(lambda e: e.activation(out=lnv[:, 0:L], in_=bK.t[:, 128:128 + L], func=AF.Square), [bK], [lnv])
            yield
            dve(lambda e: e.tensor_tensor(out=lnv[:, 0:L], in0=bK.t[:, 256:256 + L], in1=lnv[:, 0:L], op=ALU.subtract), [bK, lnv], [lnv])
            yield
            act(lambda e: e.activation(out=lnr[:, 0:L], in_=lnv[:, 0:L], func=AF.Ln, bias=epsc[:, 2:3]), [lnv, epsc], [lnr])
            yield
            act(lambda e: e.activation(out=lnr[:, 0:L], in_=lnr[:, 0:L], func=AF.Exp, scale=-0.5), [lnr], [lnr])
            yield
            for cc in range(2):
                dve(lambda e, cc=cc: e.tensor_tensor(out=lnd[:, cc, 0:L], in0=ycs[:, cc, 0:L], in1=bK.t[:, 128:128 + L], op=ALU.subtract), [ycs, bK], [lnd])
                yield
                dve(lambda e, cc=cc: e.tensor_tensor(out=lnd[:, cc, 0:L], in0=lnd[:, cc, 0:L], in1=lnr[:, 0:L], op=ALU.mult), [lnd, lnr], [lnd])
                yield
                dve(lambda e, cc=cc: e.tensor_scalar(out=lnd[:, cc, 0:L], in0=lnd[:, cc, 0:L], scalar1=pfm[:, 90 + cc:91 + cc], scalar2=pfm[:, 92 + cc:93 + cc],
                                                     op0=ALU.mult, op1=ALU.add), [lnd, pfm], [lnd])
                yield
            act(lambda e: e.activation(out=tl[:, :, 0:L], in_=lnd[:, :, 0:L], func=AF.Exp, scale=-1.0), [lnd], [tl])
            yield
            act(lambda e: e.activation(out=tl[:, :, 0:L], in_=tl[:, :, 0:L], func=AF.Ln, bias=onesf[:, 0:1]), [tl, onesf], [tl])
            yield
            act(lambda e: e.activation(out=tl[:, :, 0:L], in_=tl[:, :, 0:L], func=AF.Exp, scale=-1.0), [tl], [tl])
            yield
            dve(lambda e: e.tensor_tensor(out=mixd[:, :, 0:L], in0=tl[:, :, 0:L], in1=lnd[:, :, 0:L], op=ALU.mult), [tl, lnd], [mixd])
            yield


        def gen_swa():
            blocks = []
            if kind == "p":
                if c.has_prev:
                    blocks.append((kTp, vp, 128, biasP[:, :, :]))
                blocks.append((kTc, vc, 128, biasC[:, :, :]))
            else:
                blocks.append((kTp, vp, 128, biasSc[:, :, :]))
                blocks.append((kTc, vc, L, biasSn[:, :, :]))
            sb_banks = [bQ, bC]
            for bi, (kT_b, v_b, Lk, bias_ap) in enumerate(blocks):
                bk = sb_banks[bi]
                def smm(e, kT_b=kT_b, Lk=Lk, bk=bk):
                    for h in range(4):
                        ins = e.matmul(bk.t[0:Lk, h * L:(h + 1) * L], lhsT=kT_b[:, h // 2, 0:Lk], rhs=qT[:, h, 0:L], start=True, stop=True)
                    return ins
                pe(smm, [kT_b, qT], [bk])
                yield
                st_ = stmp[bi]; E_ = Eb[bi]
                dve(lambda e, bk=bk, Lk=Lk, st_=st_, bias_ap=bias_ap: e.scalar_tensor_tensor(
                    out=st_[0:Lk, :, 0:L], in0=bk.t[0:Lk, 0:4 * L].rearrange("p (h t) -> p h t", h=4), scalar=0.125,
                    in1=bias_ap[0:Lk, :, 0:L], op0=ALU.mult, op1=ALU.add), [bk, biasP, biasC, biasSc, biasSn], [st_])
                yield
                act(lambda e, st_=st_, E_=E_, Lk=Lk: e.activation(out=E_[0:Lk, :, 0:L], in_=st_[0:Lk, :, 0:L], func=AF.Exp), [st_], [E_])
                yield
            nb = len(blocks)
            def pvmm(e):
                for h in range(4):
                    for bi, (kT_b, v_b, Lk, _) in enumerate(blocks):
                        e.matmul(bQ.t[0:64, h * L:(h + 1) * L], lhsT=v_b[0:Lk, h // 2, :], rhs=Eb[bi][0:Lk, h, 0:L], start=(bi == 0), stop=(bi == nb - 1))
                for h in range(4):
                    for bi, (kT_b, v_b, Lk, _) in enumerate(blocks):
                        ins = e.matmul(bC.t[0:64, h * L:(h + 1) * L], lhsT=onesb[0:Lk, 0:64], rhs=Eb[bi][0:Lk, h, 0:L], start=(bi == 0), stop=(bi == nb - 1))
                return ins
            pe(pvmm, [b[1] for b in blocks] + [Eb[0], Eb[1], onesb], [bQ, bC])
            yield
            dve(lambda e: e.tensor_tensor(out=rden[:, :, 0:L], in0=bC.t[0:64, 0:4 * L].rearrange("p (h t) -> p h t", h=4),
                                          in1=esink[0:64, :].unsqueeze(2).to_broadcast([64, 4, L]), op=ALU.add), [bC, esink], [rden])
            yield
            act(lambda e: e.activation(out=rden[:, :, 0:L], in_=rden[:, :, 0:L], func=AF.Ln), [rden], [rden])
            yield
            act(lambda e: e.activation(out=rden[:, :, 0:L], in_=rden[:, :, 0:L], func=AF.Exp, scale=-1.0), [rden], [rden])
            yield
            dve(lambda e: e.tensor_tensor(out=mixb[:, :, 0:L], in0=bQ.t[0:64, 0:4 * L].rearrange("p (h t) -> p h t", h=4), in1=rden[:, :, 0:L], op=ALU.mult),
                [bQ, rden], [mixb])
            yield


        if pre_chain is not None:
            pre_chain()
        gens = [gen_dt(), gen_ssd(), gen_swa(), gen_cc()]
        while gens:
            for g_ in list(gens):
                try:
                    next(g_)
                except StopIteration:
                    gens.remove(g_)
        if next_A is not None:
            next_A()
        cutt(10, ti)
        def opmm(e):
            for half in range(2):
                bk = (banks[4], banks[5])[half]
                o = bk.t[0:L, :]
                cs = slice(half * 512, (half + 1) * 512)
                e.matmul(o, lhsT=mixa[:, 0, 0:L], rhs=w_out_m[:, 0, cs], start=True, stop=False)
                e.matmul(o, lhsT=mixa[:, 1, 0:L], rhs=w_out_m[:, 1, cs], start=False, stop=False)
                for h in range(4):
                    e.matmul(o, lhsT=mixb[:, h, 0:L], rhs=w_out_b[:, h, cs], start=False, stop=False)
                e.matmul(o, lhsT=mixc[:, 0, 0:L], rhs=w_out_m[:, 2, cs], start=False, stop=False)
                e.matmul(o, lhsT=mixc[:, 1, 0:L], rhs=w_out_m[:, 3, cs], start=False, stop=False)
                e.matmul(o, lhsT=mixd[:, 0, 0:L], rhs=w_out_m[:, 4, cs], start=False, stop=False)
                ins = e.matmul(o, lhsT=mixd[:, 1, 0:L], rhs=w_out_m[:, 5, cs], start=False, stop=True)
            return ins
        pe(opmm, [mixa, mixb, mixc, mixd, WB], [banks[4], banks[5]])
        dve(lambda e: e.tensor_tensor(out=xt[0:L, 0:512], in0=xt[0:L, 0:512], in1=banks[4].t[0:L, :], op=ALU.add), [xt, banks[4]], [xt])
        dve(lambda e: e.tensor_tensor(out=xt[0:L, 512:1024], in0=xt[0:L, 512:1024], in1=banks[5].t[0:L, :], op=ALU.add), [xt, banks[5]], [xt])
        dma("sp", x_dst, xt[0:L, :], [xt], [x_dst_b])
        c.par = 1 - c.par
        c.has_prev = True
        cutt(11, ti)

    bg_sb = sbm([128, 2, 128], F32, "bg_sb")

    FT = 256
    xf2 = [sbf([128, FT // 128, D], F32, "xf") for _ in range(2)]
    hn = sbf([128, D], BF16, "hn")
    hfT = sbf([128, 8, FT], BF16, "hfT")
    hhT = sbf([128, NFF, FT], BF16, "hhT")
    tgt = [sbf([128, FT], F32, "tgt") for _ in range(2)]
    yo = [sbf([128, D], F32, "yo") for _ in range(2)]

    def ffn_head(it):
        l, nsub, Ls, srcs, dsts, final, xf = it
        T = (nsub - 1) * 128 + Ls if nsub > 1 else Ls
        for si in range(nsub):
            Lc = 128 if nsub > 1 else Ls
            dma("pool", xf[0:Lc, si, :], srcs[si][0], [srcs[si][1]], [xf])
        for si in range(nsub):
            Lc = 128 if nsub > 1 else Ls
            rmsnorm_tile(xf, xf[0:Lc, si, :], Lc, gffn, hn[0:Lc, :], hn, 2)
            T0 = banks[0]; T0v = bf(T0)
            def tr8(e, Lc=Lc):
                for k in range(8):
                    ins = e.transpose(T0v[:, k * 128:k * 128 + Lc], hn[0:Lc, k * 128:(k + 1) * 128], identb[0:Lc, 0:Lc])
                return ins
            pe(tr8, [hn, identb], [T0])
            act(lambda e, si=si, Lc=Lc: e.activation(out=hfT[:, :, si * 128:si * 128 + Lc],
                                                     in_=T0v.rearrange("p (k t) -> p k t", k=8)[:, :, 0:Lc], func=AF.Copy), [T0], [hfT])

    def ffn_body(it):
        l, nsub, Ls, srcs, dsts, final, xf = it
        T = (nsub - 1) * 128 + Ls if nsub > 1 else Ls
        for j in range(NFF):
            bg_ = banks[1 + (j % 2)]; bu_ = banks[3 + (j % 2)]
            def gu(e, j=j, bg_=bg_, bu_=bu_):
                for k in range(8):
                    e.matmul(bg_.t[:, 0:T], lhsT=wg_sb[:, k, j * 128:(j + 1) * 128], rhs=hfT[:, k, 0:T], start=(k == 0), stop=(k == 7))
                for k in range(8):
                    ins = e.matmul(bu_.t[:, 0:T], lhsT=wu_sb[:, k, j * 128:(j + 1) * 128], rhs=hfT[:, k, 0:T], start=(k == 0), stop=(k == 7))
                return ins
            pe(gu, [WF[[gi for gi, (j0, j1) in enumerate(FFN_GROUPS) if j0 <= j < j1][0]], hfT], [bg_, bu_])
            t_ = tgt[j % 2]
            act(lambda e, t_=t_, bg_=bg_: e.activation(out=t_[:, 0:T], in_=bg_.t[:, 0:T], func=AF.Silu), [bg_], [t_])
            dve(lambda e, t_=t_, bu_=bu_, j=j: e.tensor_tensor(out=hhT[:, j, 0:T], in0=t_[:, 0:T], in1=bu_.t[:, 0:T], op=ALU.mult), [t_, bu_], [hhT])

    def ffn_tail(it):
        l, nsub, Ls, srcs, dsts, final, xf = it
        for si in range(nsub):
            Lc = 128 if nsub > 1 else Ls
            for half in range(2):
                bk = banks[5 + half]
                def dn(e, si=si, half=half, bk=bk, Lc=Lc):
                    for j in range(NFF):
                        ins = e.matmul(bk.t[0:Lc, :], lhsT=hhT[:, j, si * 128:si * 128 + Lc], rhs=wd_sb[:, j, half * 512:(half + 1) * 512],
                                       start=(j == 0), stop=(j == NFF - 1))
                    return ins
                pe(dn, [hhT, WB], [bk])
                dve(lambda e, si=si, half=half, bk=bk, Lc=Lc: e.scalar_tensor_tensor(
                    out=xf[0:Lc, si, half * 512:(half + 1) * 512], in0=bk.t[0:Lc, :], scalar=1.0, in1=xf[0:Lc, si, half * 512:(half + 1) * 512],
                    op0=ALU.mult, op1=ALU.add), [bk, xf], [xf])
            if final:
                y_ = yo[si % 2]
                dve(lambda e, Lc=Lc: e.memset(ss[0:Lc, 3:4], 0.0), [], [ss])
                act(lambda e, si=si, Lc=Lc: e.activation(out=junk[0:Lc, :], in_=xf[0:Lc, si, :], func=AF.Square, accum_out=ss[0:Lc, 3:4]), [xf, ss], [junk, ss])
                rsqrt_col(rs[0:Lc, 3:4], ss[0:Lc, 3:4], (rs, ss), 1.0, D * EPS)
                dve(lambda e, si=si, Lc=Lc, y_=y_: e.scalar_tensor_tensor(out=y_[0:Lc, :], in0=xf[0:Lc, si, :], scalar=rs[0:Lc, 3:4], in1=gfin[0:Lc, :],
                                                                          op0=ALU.mult, op1=ALU.mult), [xf, rs, gfin], [y_])
                dma("sp", dsts[si][0], y_[0:Lc, :], [y_], [dsts[si][1]] if dsts[si][1] is not None else [])
            else:
                dma("sp", dsts[si][0], xf[0:Lc, si, :], [xf], [dsts[si][1]])

    import os
    KSTOP = int(os.environ.get("KSTOP", "0"))

    class _Stop(Exception):
        pass

    def cut(n):
        if KSTOP == n:
            raise _Stop()

    KSUB = int(os.environ.get("KSUB", "0"))
    KTILE = int(os.environ.get("KTILE", "0"))

    def cutt(n, ti):
        if KSTOP == n and ti == KTILE:
            raise _Stop()

    def cutn(n, ti):
        if KSUB == n and ti == KTILE:
            raise _Stop()

    def schedule():
        for l in range(DEPTH):
            P.fence()
            load_layer_params(l)
            cut(1)
            load_mixer_weights(l)
            cut(2)
            init_prompt_ctx()
            src_p, src_s = (xp, xs) if l == 0 else (xb_p, xb_s)
            tiles = []
            for i in range(NPT):
                sbuf_ = None if l == 0 else dB(("xb_p", i))
                tiles.append(dict(c=ctxP, L=128, src=src_p[i * 128:(i + 1) * 128, :], sb=sbuf_, dst=xa_p[i * 128:(i + 1) * 128, :],
                                  db=dB(("xa_p", i)), kind="p", s=0, last=(i == NPT - 1)))
            for s in range(4):
                sbuf_ = None if l == 0 else dB(("xb_s", 0))
                tiles.append(dict(c=ctxS, L=16, src=src_s[s * 16:(s + 1) * 16, :], sb=sbuf_, dst=xa_s[s * 16:(s + 1) * 16, :],
                                  db=dB(("xa_s", 0)), kind="s", s=s, last=True))
            mixer_head_norm(tiles[0]["L"], tiles[0]["src"], tiles[0]["sb"], 0)
            mixer_head_tr(tiles[0]["L"], 0)
            for ti, t in enumerate(tiles):
                if t["kind"] == "s":
                    if t["s"] == 0:
                        cut(12)
                    init_sample_ctx(l, t["s"])
                    cut(13)
                nxt = tiles[ti + 1] if ti + 1 < len(tiles) else None
                hook = (lambda nxt=nxt, ti=ti: mixer_head_tr(nxt["L"], ti + 1)) if nxt is not None else None
                hook0 = (lambda nxt=nxt, ti=ti: mixer_head_norm(nxt["L"], nxt["src"], nxt["sb"], ti + 1)) if nxt is not None else None
                hoist = nxt is not None and nxt["kind"] == "p"
                hookA = (lambda nxt=nxt, ti=ti: mixer_front_A(nxt["c"], nxt["L"], ti + 1)) if hoist else None
                mixer_tile(l, t["c"], t["L"], t["src"], t["sb"], t["dst"], t["db"], t["kind"], t["s"], t["last"], ti, hook, hook0,
                           skip_A=(ti > 0 and t["kind"] == "p"), next_A=hookA)
                if t["kind"] == "s":
                    cut(14)
            cut(15)
            P.fence()
            load_ffn_weights(l)
            cut(16)
            final = (l == DEPTH - 1)
            NSUB = FT // 128
            nmt = (NPT + NSUB - 1) // NSUB
            items = []
            for m in range(nmt):
                subs = list(range(m * NSUB, min(NPT, m * NSUB + NSUB)))
                srcs = [(xa_p[i * 128:(i + 1) * 128, :], dB(("xa_p", i))) for i in subs]
                if final:
                    dsts = [(yp[i * 128:(i + 1) * 128, :], None) for i in subs]
                else:
                    dsts = [(xb_p[i * 128:(i + 1) * 128, :], dB(("xb_p", i))) for i in subs]
                items.append((l, len(subs), 128, srcs, dsts, final, xf2[len(items) % 2]))
            dsts = [(ys[:, :], None)] if final else [(xb_s[:, :], dB(("xb_s", 0)))]
            items.append((l, 1, 64, [(xa_s[:, :], dB(("xa_s", 0)))], dsts, final, xf2[len(items) % 2]))
            ffn_head(items[0])
            for ii, it in enumerate(items):
                ffn_body(it)
                if ii + 1 < len(items):
                    ffn_head(items[ii + 1])
                ffn_tail(it)
                cut(17)
            cut(18)


    try:
        schedule()
    except _Stop:
        P.fence()
        dma("sp", yp[0:1, 0:8], ss[0:1, 0:8], [], [])

    P.finalize(lambda n: es.enter_context(nc.semaphore(n)))
    with nc.Block() as block:
        @block.tensor
        def _(e):
            P.run_engine("pe", e)

        @block.scalar
        def _(e):
            P.run_engine("act", e)

        @block.vector
        def _(e):
            P.run_engine("dve", e)

        @block.gpsimd
        def _(e):
            P.run_engine("pool", e)

        @block.sync
        def _(e):
            P.run_engine("sp", e, last=True)
    build.sbuf_left = nc.sbuf_bytes_remaining
    es.close()
    return nc, P


_CACHE = {}


def _run(inputs, NPT, DEPTH=2, ncores=8, trace=False):
    key = (NPT, DEPTH)
    if key not in _CACHE:
        _CACHE[key] = build(NPT, DEPTH)[0]
    nc = _CACHE[key]
    f = lambda a: np.ascontiguousarray(np.asarray(a, dtype=np.float32))
    I = {k: f(v) for k, v in inputs.items()}
    consts = _consts()
    pfm, prow = _pack_params(I["ssm_conv_w"], I["ssm_conv_b"], I["ssm_dt_bias"], I["ssm_a_log"], I["ssm_d"], I["ssm_norm"],
                             I["swa_sinks"], I["sconv_w"], I["cconv_w"], I["cconv_b"], I["cconv_ln_g"], I["cconv_ln_b"])
    in_maps = []
    for c in range(ncores):
        b = c % 4
        sl = slice(4 * c, 4 * c + 4)
        m = {
            "xp": f(I["x_prompt"][b]), "xs": f(I["x_sample"][sl].reshape(64, D)),
            "st_ssm": f(I["state_ssm"][:, sl].reshape(DEPTH, 4, 256, 128)),
            "st_xc": f(I["state_ssm_conv"][:, sl]),
            "ck": f(I["cache_swa_k"][:, sl].reshape(DEPTH, 4, 128, 128)),
            "cv": f(I["cache_swa_v"][:, sl].reshape(DEPTH, 4, 128, 128)),
            "st_sc": f(I["state_sconv"][:, sl]), "st_cc": f(I["state_cconv"][:, sl]),
            "w_in": I["w_in"], "w_out": I["w_out"], "w_gate": I["w_gate"], "w_up": I["w_up"], "w_down": I["w_down"],
            "norm_mix": I["norm_mix"], "norm_ffn": I["norm_ffn"], "norm_final": f(I["norm_final"].reshape(1, D)),
            "pfm": pfm, "prow": prow, "ssm_conv_b": I["ssm_conv_b"],
        }
        m.update(consts)
        in_maps.append(m)
    res = run_bass_kernel_spmd(nc, in_maps, core_ids=list(range(ncores)))
    R = res.results
    nb = 4
    S = NPT * 128
    y_prompt = np.stack([R[b]["yp"] for b in range(nb)], 0)
    y_sample = np.concatenate([R[c]["ys"].reshape(4, 16, D) for c in range(ncores)], 0)

    def pstack(name, shp):
        return np.stack([R[b][name].reshape((DEPTH,) + shp) for b in range(nb)], 1)

    def sstack(name, shp):
        return np.concatenate([R[c][name].reshape((DEPTH, 4) + shp) for c in range(ncores)], 1)

    outs = (y_prompt, y_sample,
            pstack("o_ssm_p", (4, 64, 128)), sstack("o_ssm_s", (4, 64, 128)),
            pstack("o_xc_p", (3, 512)), sstack("o_xc_s", (3, 512)),
            pstack("o_k_p", (128, 2, 64)), sstack("o_k_s", (16, 2, 64)),
            pstack("o_v_p", (128, 2, 64)), sstack("o_v_s", (16, 2, 64)),
            pstack("o_sc_p", (2, 256)), sstack("o_sc_s", (2, 256)),
            pstack("o_cc_p", (30, 256)), sstack("o_cc_s", (30, 256)))
    return tuple(np.ascontiguousarray(o.astype(np.float32)) for o in outs)


def kernel(**inputs):
    return _run(inputs, NPT=32, DEPTH=2)
```

```python
import numpy as np
from contextlib import ExitStack
import concourse.bass as bass
import concourse.mybir as mybir
from concourse.bass_utils import run_bass_kernel_spmd

F32 = mybir.dt.float32
BF16 = mybir.dt.bfloat16
ALU = mybir.AluOpType
AF = mybir.ActivationFunctionType
AX = mybir.AxisListType

COMPUTE = ("pe", "act", "dve", "pool")
QUEUES = ("sp", "pool")
DMA_POOL = 24
EPS = 1e-5
D = 1024
DIN = 2564
DFF = 2816
NFF = 22
NEG = -30000.0


class Buf:
    __slots__ = ("name", "lw", "rd_c", "rd_d", "excl")

    def __init__(self, name, excl=False):
        self.name = name
        self.lw = None
        self.rd_c = {}
        self.rd_d = []
        self.excl = excl


class Op:
    __slots__ = ("eng", "fn", "deps", "dma", "idx", "signal", "cnt", "sem", "target", "waits", "gidx")

    def __init__(self, eng, fn, dma):
        self.eng = eng
        self.fn = fn
        self.dma = dma
        self.deps = []
        self.signal = False
        self.cnt = 0
        self.sem = None
        self.target = 0
        self.waits = []


class Prog:
    def __init__(self, nc):
        self.nc = nc
        self.ops = {e: [] for e in ("pe", "act", "dve", "pool", "sp")}
        self.all = []
        self.fence_pending = {}

    def fence(self):
        f = []
        for e in COMPUTE:
            if self.ops[e]:
                f.append(self.ops[e][-1])
        f.extend(op for op in self.all if op.dma and op.gidx >= getattr(self, "_fence_g", 0))
        self._fence_g = len(self.all)
        self.fence_pending = {e: f for e in self.ops}

    def add(self, eng, fn, reads=(), writes=(), dma=False):
        op = Op(eng, fn, dma)
        op.idx = len(self.ops[eng])
        op.gidx = len(self.all)
        deps = {}

        def add_dep(d):
            if d is None or d is op:
                return
            if d.dma:
                deps[("dma", d.gidx)] = d
            else:
                k = ("c", d.eng)
                if k not in deps or deps[k].idx < d.idx:
                    deps[k] = d

        if eng in self.fence_pending:
            for d in self.fence_pending.pop(eng):
                add_dep(d)
        for b in reads:
            add_dep(b.lw)
            if b.excl:
                for e2, d in b.rd_c.items():
                    if e2 != eng:
                        add_dep(d)
        for b in writes:
            if b.lw is not None and (dma or b.lw.dma or b.lw.eng != eng):
                add_dep(b.lw)
            for e2, d in b.rd_c.items():
                if dma or e2 != eng:
                    add_dep(d)
            for d in b.rd_d:
                add_dep(d)
        for b in reads:
            if dma:
                b.rd_d.append(op)
            else:
                b.rd_c[eng] = op
        for b in writes:
            b.lw = op
            b.rd_c = {}
            b.rd_d = []
        op.deps = list(deps.values())
        self.ops[eng].append(op)
        self.all.append(op)
        return op

    @staticmethod
    def _skip(d, op):
        return d.eng == op.eng and op.eng == "pe" and not op.dma and not d.dma

    def finalize(self, sem_ctx):
        esem = {e: sem_ctx(f"c_{e}") for e in COMPUTE}
        dsem = {q: [sem_ctx(f"d_{q}{i}") for i in range(DMA_POOL)] for q in QUEUES}
        dcount = {q: [0] * DMA_POOL for q in QUEUES}
        dk = {q: 0 for q in QUEUES}
        for op in self.all:
            for d in op.deps:
                if d.dma or self._skip(d, op):
                    continue
                d.signal = True
        cnt = {e: 0 for e in COMPUTE}
        for op in self.all:
            if op.dma:
                q = op.eng
                k = dk[q] % DMA_POOL
                dk[q] += 1
                dcount[q][k] += 1
                op.sem = dsem[q][k]
                op.target = 16 * dcount[q][k]
                op.cnt = (q, k)
            elif op.signal:
                cnt[op.eng] += 1
                op.cnt = cnt[op.eng]
        for e, lst in self.ops.items():
            w = {}
            for op in lst:
                ws = []

                def need(sem, key, val):
                    if w.get(key, 0) >= val:
                        return
                    w[key] = val
                    ws.append((sem, val))

                if op.dma and op.target > 16:
                    need(op.sem, ("d",) + op.cnt, op.target - 16)
                for d in op.deps:
                    if d.dma:
                        need(d.sem, ("d",) + d.cnt, d.target)
                    elif not self._skip(d, op):
                        need(esem[d.eng], ("c", d.eng), d.cnt)
                op.waits = ws
        self._esem = esem
        self._final = [(dsem[q][k], 16 * dcount[q][k]) for q in QUEUES for k in range(DMA_POOL) if dcount[q][k] > 0]
        self.sig_counts = cnt

    def run_engine(self, e, eng, last=False):
        esem = self._esem
        for op in self.ops[e]:
            for sem, val in op.waits:
                eng.wait_ge(sem, val)
            ins = op.fn(eng)
            if op.dma:
                ins.then_inc(op.sem, 16)
            elif op.signal:
                ins.then_inc(esem[e], 1)
        if last:
            for sem, val in self._final:
                eng.wait_ge(sem, val)


class TT:
    def __init__(self, t, buf):
        self.t = t
        self.b = buf

    def __getitem__(self, k):
        return self.t[k]


def _consts():
    c = {}
    c["ident"] = np.eye(128, dtype=np.float32)
    k = np.arange(128)
    c["utri"] = (k[:, None] <= k[None, :]).astype(np.float32)
    mb = np.where(k[None, :] >= k[:, None], 0.0, NEG).astype(np.float32)
    c["maskb4"] = np.ascontiguousarray(np.broadcast_to(mb[:, None, :], (128, 4, 128))).astype(np.float32)
    slopes = (2.0 ** (-8.0 * np.arange(1, 5, dtype=np.float32) / 4.0)).astype(np.float32)
    j = np.arange(128)[:, None, None].astype(np.float32)
    i = np.arange(128)[None, None, :].astype(np.float32)
    sl = slopes[None, :, None]
    bp = -sl * np.abs(i - (j - 128.0))
    bp = np.where((j < 64) & (i >= 64), NEG, bp)
    c["biasP"] = np.ascontiguousarray(bp).astype(np.float32)
    bc = -sl * np.abs(i - j)
    bc = np.where((j >= 64) & (i < 64), NEG, bc)
    c["biasC"] = np.ascontiguousarray(bc).astype(np.float32)
    i16 = np.arange(16)[None, None, :].astype(np.float32)
    c["biasSc"] = np.ascontiguousarray(-sl * np.abs((1024.0 + i16) - (896.0 + j))).astype(np.float32)
    j16 = np.arange(16)[:, None, None].astype(np.float32)
    c["biasSn"] = np.ascontiguousarray(-sl * np.abs(i16 - j16)).astype(np.float32)
    return c


NPF = 94
NPR = 272


def _pack_params(ssm_conv_w, ssm_conv_b, ssm_dt_bias, ssm_a_log, ssm_d, ssm_norm, swa_sinks, sconv_w, cconv_w,
                 cconv_b, cconv_ln_g, cconv_ln_b):
    Ld = ssm_conv_w.shape[0]
    pfm = np.zeros((Ld, 128, NPF), np.float32)
    prow = np.zeros((Ld, NPR), np.float32)
    for l in range(Ld):
        o = 0
        pfm[l, :, o:o + 16] = ssm_conv_w[l].reshape(4, 4, 128).transpose(2, 1, 0).reshape(128, 16); o += 16
        pfm[l, :, o:o + 4] = ssm_conv_b[l].reshape(4, 128).T; o += 4
        pfm[l, :, o:o + 6] = sconv_w[l].reshape(3, 2, 128).transpose(2, 1, 0).reshape(128, 6); o += 6
        pfm[l, :, o:o + 62] = cconv_w[l].reshape(31, 2, 128).transpose(2, 1, 0).reshape(128, 62); o += 62
        pfm[l, :, o:o + 2] = cconv_b[l].reshape(2, 128).T; o += 2
        pfm[l, :, o:o + 2] = cconv_ln_g[l].reshape(2, 128).T; o += 2
        pfm[l, :, o:o + 2] = cconv_ln_b[l].reshape(2, 128).T; o += 2
        prow[l, 0:4] = ssm_dt_bias[l]
        prow[l, 4:8] = ssm_a_log[l]
        prow[l, 8:12] = ssm_d[l]
        prow[l, 12:16] = swa_sinks[l]
        prow[l, 16:272] = ssm_norm[l]
    return pfm, prow


def build(NPT, DEPTH=2):
    nc = bass.Bass("TRN2", target_bir_lowering=False)
    es = ExitStack()
    P = Prog(nc)
    NTP = NPT * 128

    def din(name, shape):
        return nc.dram_tensor(name, list(shape), F32, kind="ExternalInput").ap()

    def dout(name, shape):
        return nc.dram_tensor(name, list(shape), F32, kind="ExternalOutput").ap()

    xp = din("xp", [NTP, D]); xs = din("xs", [64, D])
    st_ssm = din("st_ssm", [DEPTH, 4, 256, 128]); st_xc = din("st_xc", [DEPTH, 4, 3, 512])
    ck = din("ck", [DEPTH, 4, 128, 128]); cv = din("cv", [DEPTH, 4, 128, 128])
    st_sc = din("st_sc", [DEPTH, 4, 2, 256]); st_cc = din("st_cc", [DEPTH, 4, 30, 256])
    w_in = din("w_in", [DEPTH, D, DIN]); w_out = din("w_out", [DEPTH, D, D])
    w_gate = din("w_gate", [DEPTH, D, DFF]); w_up = din("w_up", [DEPTH, D, DFF]); w_down = din("w_down", [DEPTH, DFF, D])
    norm_mix = din("norm_mix", [DEPTH, D]); norm_ffn = din("norm_ffn", [DEPTH, D]); norm_final = din("norm_final", [1, D])
    pfm_d = din("pfm", [DEPTH, 128, NPF]); prow_d = din("prow", [DEPTH, NPR]); ssm_conv_b_d = din("ssm_conv_b", [DEPTH, 512])
    c_ident = din("ident", [128, 128]); c_utri = din("utri", [128, 128]); c_maskb4 = din("maskb4", [128, 4, 128])
    c_biasP = din("biasP", [128, 4, 128]); c_biasC = din("biasC", [128, 4, 128])
    c_biasSc = din("biasSc", [128, 4, 16]); c_biasSn = din("biasSn", [16, 4, 16])

    yp = dout("yp", [NTP, D]); ys = dout("ys", [64, D])
    o_ssm_p = dout("o_ssm_p", [DEPTH, 256, 128]); o_ssm_s = dout("o_ssm_s", [DEPTH, 4, 256, 128])
    o_xc_p = dout("o_xc_p", [DEPTH, 3, 512]); o_xc_s = dout("o_xc_s", [DEPTH, 4, 3, 512])
    o_k_p = dout("o_k_p", [DEPTH, 128, 128]); o_k_s = dout("o_k_s", [DEPTH, 4, 16, 128])
    o_v_p = dout("o_v_p", [DEPTH, 128, 128]); o_v_s = dout("o_v_s", [DEPTH, 4, 16, 128])
    o_sc_p = dout("o_sc_p", [DEPTH, 2, 256]); o_sc_s = dout("o_sc_s", [DEPTH, 4, 2, 256])
    o_cc_p = dout("o_cc_p", [DEPTH, 30, 256]); o_cc_s = dout("o_cc_s", [DEPTH, 4, 30, 256])

    xa_p = nc.dram_tensor("xa_p", [NTP, D], F32, kind="Internal").ap()
    xa_s = nc.dram_tensor("xa_s", [64, D], F32, kind="Internal").ap()
    xb_p = nc.dram_tensor("xb_p", [NTP, D], F32, kind="Internal").ap()
    xb_s = nc.dram_tensor("xb_s", [64, D], F32, kind="Internal").ap()
    dbuf = {}

    def dB(key):
        if key not in dbuf:
            dbuf[key] = Buf(str(key))
        return dbuf[key]

    _n = [0]

    def sb(shape, dt=F32, name=None):
        _n[0] += 1
        nm = f"{name or 't'}_{_n[0]}"
        return TT(es.enter_context(nc.sbuf_tensor(nm, list(shape), dt)), Buf(nm))

    class Arena:
        def __init__(self, base_ap, lo, hi):
            self.base = base_ap; self.lo = lo; self.hi = hi; self.cur = lo

        def reset(self):
            self.cur = self.lo

        def alloc(self, shape, dt, name):
            n = 1
            for d_ in shape[1:]:
                n *= d_
            nb16 = n * (2 if dt == F32 else 1)
            nb16 += nb16 % 2
            if self.cur + nb16 > self.hi:
                return None
            v = self.base[:, self.cur:self.cur + n * (2 if dt == F32 else 1)]
            self.cur += nb16
            if dt == F32:
                v = v.bitcast(F32)
            if len(shape) == 3:
                v = v.rearrange("p (a b) -> p a b", a=shape[1])
            if shape[0] < 128:
                v = v[0:shape[0]]
            return TT(v, Buf(name))

    arenas = {}

    def sbm(shape, dt=F32, name="m"):
        for a in arenas["m"]:
            r = a.alloc(shape, dt, name)
            if r is not None:
                return r
        return sb(shape, dt, name)

    def sbf(shape, dt=F32, name="f"):
        r = arenas["f"][0].alloc(shape, dt, name)
        assert r is not None, name
        return r

    banks = []
    for i in range(8):
        banks.append(TT(es.enter_context(nc.psum_tensor(f"bank{i}", [128, 512], F32)), Buf(f"bank{i}", excl=True)))

    def bf(bank):
        return bank.t[:].bitcast(BF16)

    def _bl(lst):
        out = []
        for x in lst:
            if isinstance(x, TT):
                out.append(x.b)
            elif isinstance(x, list):
                out.extend(x)
            else:
                out.append(x)
        return out

    def pe(fn, r, w):
        P.add("pe", fn, _bl(r), _bl(w))

    def act(fn, r, w):
        P.add("act", fn, _bl(r), _bl(w))

    def dve(fn, r, w):
        P.add("dve", fn, _bl(r), _bl(w))

    def pool(fn, r, w):
        P.add("pool", fn, _bl(r), _bl(w))

    def dma(q, out, in_, r, w):
        P.add(q, lambda e, out=out, in_=in_: e.dma_start(out=out, in_=in_), _bl(r), _bl(w), dma=True)

    WA = []
    WB = []

    def wdma(out, in_, lst=None):
        b = Buf("w")
        (WA if lst is None else lst).append(b)
        dma("pool", out, in_, [], [b])

    ident = sb([128, 128], F32, "ident"); identb = sb([128, 128], BF16, "identb")
    utri = sb([128, 128], F32, "utri"); onesf = sb([128, 128], F32, "onesf"); onesb = sb([128, 128], BF16, "onesb")
    ones256 = sb([128, 128], F32, "ones256")
    neghalf = sb([128, 128], F32, "neghalf")
    epsc = sb([128, 4], F32, "epsc")
    maskb4 = sb([128, 4, 128], F32, "maskb4")
    biasP = sb([128, 4, 128], F32, "biasP"); biasC = sb([128, 4, 128], F32, "biasC")
    biasSc = sb([128, 4, 16], F32, "biasSc"); biasSn = sb([16, 4, 16], F32, "biasSn")
    dma("sp", ident[:], c_ident, [], [ident]); dma("sp", utri[:], c_utri, [], [utri])
    dma("sp", maskb4[:], c_maskb4, [], [maskb4]); dma("sp", biasP[:], c_biasP, [], [biasP])
    dma("sp", biasC[:], c_biasC, [], [biasC]); dma("sp", biasSc[:], c_biasSc, [], [biasSc])
    dma("sp", biasSn[:], c_biasSn, [], [biasSn])
    dve(lambda e: e.tensor_copy(out=identb[:], in_=ident[:]), [ident], [identb])
    dve(lambda e: e.memset(onesf[:], 1.0), [], [onesf])
    dve(lambda e: e.memset(onesb[:], 1.0), [], [onesb])
    dve(lambda e: e.memset(ones256[:], 1.0 / 256.0), [], [ones256])
    dve(lambda e: e.memset(neghalf[:], -0.5), [], [neghalf])
    dve(lambda e: e.memset(epsc[:, 0:1], D * EPS), [], [epsc])
    dve(lambda e: e.memset(epsc[:, 1:2], 256.0 * EPS), [], [epsc])
    dve(lambda e: e.memset(epsc[:, 2:3], EPS), [], [epsc])

    gmix = sb([128, D], F32, "gmix"); gffn = sb([128, D], F32, "gffn"); gfin = sb([128, D], F32, "gfin")
    pfm = sb([128, NPF], F32, "pfm"); pfh = sb([128, NPF], F32, "pfh")
    prow = sb([128, NPR], F32, "prow")
    Aneg = sb([128, 4], F32, "Aneg"); esink = sb([128, 4], F32, "esink"); g16 = sb([128, 256], F32, "g16")
    dma("sp", gfin[:], norm_final.partition_broadcast(128), [], [gfin])
    dve(lambda e: e.tensor_scalar(out=gfin[:], in0=gfin[:], scalar1=32.0, scalar2=None, op0=ALU.mult), [gfin], [gfin])

    WM_IN = 8 * DIN
    warena = es.enter_context(nc.sbuf_tensor("warena", [128, 3 * 8 * DFF], BF16))
    w_in_sb = warena[:, 0:WM_IN].rearrange("p (k n) -> p k n", k=8)
    o1 = WM_IN
    w_out_m = warena[:, o1:o1 + 6 * D].rearrange("p (k n) -> p k n", k=6); o1 += 6 * D
    w_out_b = warena[0:64, o1:o1 + 4 * D].rearrange("p (k n) -> p k n", k=4); o1 += 4 * D
    diag = warena[:, o1:o1 + 62 * 128].rearrange("p (k n) -> p k n", k=62); o1 += 62 * 128
    sarena = es.enter_context(nc.sbuf_tensor("sarena", [128, 22528], BF16))
    arenas["m"] = [Arena(warena[:, :], o1, 3 * 8 * DFF), Arena(sarena[:, :], 0, 22528)]
    arenas["f"] = [Arena(sarena[:, :], 0, 22528)]
    wg_sb = warena[:, 0:8 * DFF].rearrange("p (k n) -> p k n", k=8)
    wu_sb = warena[:, 8 * DFF:16 * DFF].rearrange("p (k n) -> p k n", k=8)
    wd_sb = warena[:, 16 * DFF:24 * DFF].rearrange("p (k n) -> p k n", k=NFF)

    x_t = [sbm([128, D], F32, "x_t") for _ in range(2)]
    junk = sb([128, D], BF16, "junk")
    xn2 = [sbm([128, D], BF16, "xn") for _ in range(2)]
    ss = sb([128, 8], F32, "ss"); rs = sb([128, 8], F32, "rs")
    xnT2 = [sbm([128, 8, 128], BF16, "xnT") for _ in range(2)]
    cbx = sbm([128, 4, 131], F32, "cbx"); tnx = sbm([128, 4, 128], F32, "tnx")
    cbxb = sbm([128, 4, 132], BF16, "cbxb"); cbsb = sbm([128, 2, 130], BF16, "cbsb")
    dgx = sbm([128, 22, 128], BF16, "dgx")
    cbrow = sb([1, 512], BF16, "cbrow")
    xbcs = sbm([128, 4, 128], BF16, "xbcs")
    dtr = sbm([128, 4], F32, "dtr"); dtv = sbm([128, 4], F32, "dtv"); av = sbm([128, 4], F32, "av")
    eac = sbm([128, 4], F32, "eac"); bd = sbm([128, 4], F32, "bd")
    xs_tok = sbm([128, 256], BF16, "xs_tok"); B_tok = sbm([128, 128], BF16, "B_tok")
    xdt = sbm([128, 256], BF16, "xdt"); xdte = sbm([128, 256], BF16, "xdte")
    aU = sbm([128, 4, 128], F32, "aU"); na = sbm([128, 4, 128], F32, "na"); decay = sbm([128, 4, 128], F32, "decay")
    MT = sbm([128, 4, 128], BF16, "MT")
    y1 = sbm([128, 256], F32, "y1"); y3 = sbm([128, 256], F32, "y3"); tz = sbm([128, 256], F32, "tz"); g1 = sbm([128, 256], F32, "g1")
    ynb = sbm([128, 256], BF16, "ynb")
    mixa = sbm([128, 2, 128], BF16, "mixa"); mixb = sbm([64, 4, 128], BF16, "mixb")
    mixc = sbm([128, 2, 128], BF16, "mixc"); mixd = sbm([128, 2, 128], BF16, "mixd")
    qT = sbm([64, 4, 128], BF16, "qT")
    kv32 = sbm([128, 256], F32, "kv32")
    stmp = [sbm([128, 4, 128], F32, "stmp") for _ in range(2)]
    Eb = [sbm([128, 4, 128], BF16, "Eb") for _ in range(2)]
    rden = sbm([64, 4, 128], F32, "rden")
    h_sb = sbm([128, 2, 128], F32, "h_sb"); cbs = sbm([128, 2, 130], F32, "cbs")
    tg = sbm([128, 2, 128], F32, "tg"); glu32 = sbm([128, 2, 128], F32, "glu32")
    ycs = sbm([128, 2, 128], F32, "ycs"); ysq = sbm([128, 2, 128], F32, "ysq")
    lnv = sbm([128, 128], F32, "lnv"); lnr = sbm([128, 128], F32, "lnr"); lnd = sbm([128, 2, 128], F32, "lnd")
    tl = sbm([128, 2, 128], F32, "tl")
    hst = sbm([128, 256], F32, "hst")
    otrs = [sbm([128, 128], F32, "otr") for _ in range(4)]
    _rot = [0, 0]
    ldts = [sbm([128, 512], F32, "ldt") for _ in range(2)]

    class Ctx:
        pass

    def mkctx(nm):
        c = Ctx()
        c.hT = sbm([128, 256], F32, nm + "hT"); c.hTb = sbm([128, 256], BF16, nm + "hTb")
        c.tx = sbm([128, 4, 3], F32, nm + "tx"); c.tsc = sbm([128, 2, 2], F32, nm + "tsc")
        c.cbc = sbm([128, 2, 158], BF16, nm + "cbc")
        c.kT = [sbm([64, 2, 128], BF16, nm + "kT") for _ in range(2)]
        c.v = [sbm([128, 2, 64], BF16, nm + "v") for _ in range(2)]
        c.par = 0
        c.has_prev = False
        return c

    ctxP = mkctx("P")
    ctxS = mkctx("S")

    def transpose_to(dst_ap, dst_tt, src_ap, src_tt, R, C, bank, col0=0, eng="dve", dt32=True):
        idt = ident if dt32 else identb
        if dt32:
            pv = bank.t[0:C, col0:col0 + R]
        else:
            pv = bf(bank)[0:C, col0:col0 + R]
        pe(lambda e: e.transpose(pv, src_ap, idt[0:R, 0:R]), [src_tt, idt], [bank])
        if eng == "act":
            act(lambda e: e.activation(out=dst_ap, in_=pv, func=AF.Copy), [bank], [dst_tt])
        else:
            dve(lambda e: e.tensor_copy(out=dst_ap, in_=pv), [bank], [dst_tt])

    def rsqrt_col(dst_ap, src_ap, tts, n_mult, add_c, width=1):
        ecol = 0 if add_c == D * EPS else 1
        P_ = dst_ap.shape[0]
        act(lambda e: e.activation(out=dst_ap, in_=src_ap, func=AF.Ln, bias=epsc[0:P_, ecol:ecol + 1]), [tts[1], epsc], [tts[0]])
        act(lambda e: e.activation(out=dst_ap, in_=dst_ap, func=AF.Exp, scale=-0.5), [tts[0]], [tts[0]])

    def sigm(out_ap, out_tt, in_ap, in_tt, P_):
        act(lambda e: e.activation(out=out_ap, in_=in_ap, func=AF.Exp, scale=-1.0), [in_tt], [out_tt])
        act(lambda e: e.activation(out=out_ap, in_=out_ap, func=AF.Ln, bias=onesf[0:P_, 0:1]), [out_tt, onesf], [out_tt])
        act(lambda e: e.activation(out=out_ap, in_=out_ap, func=AF.Exp, scale=-1.0), [out_tt], [out_tt])

    def rmsnorm_tile(x_tt, x_ap, L, g_tt, out_ap, out_tt, col):
        dve(lambda e: e.memset(ss[0:L, col:col + 1], 0.0), [], [ss])
        act(lambda e: e.activation(out=junk[0:L, :], in_=x_ap, func=AF.Square, accum_out=ss[0:L, col:col + 1]),
            [x_tt, ss], [junk, ss])
        rsqrt_col(rs[0:L, col:col + 1], ss[0:L, col:col + 1], (rs, ss), 1.0, D * EPS)
        dve(lambda e: e.scalar_tensor_tensor(out=out_ap, in0=x_ap, scalar=rs[0:L, col:col + 1], in1=g_tt[0:L, :],
                                             op0=ALU.mult, op1=ALU.mult), [x_tt, rs, g_tt], [out_tt])

    def load_layer_params(l):
        dma("sp", gmix[:], norm_mix[l:l + 1, :].partition_broadcast(128), [], [gmix])
        dma("sp", gffn[:], norm_ffn[l:l + 1, :].partition_broadcast(128), [], [gffn])
        dma("sp", pfm[:], pfm_d[l], [], [pfm])
        dma("sp", prow[:], prow_d[l:l + 1, :].partition_broadcast(128), [], [prow])
        dve(lambda e: e.tensor_scalar(out=gmix[:], in0=gmix[:], scalar1=32.0, scalar2=None, op0=ALU.mult), [gmix], [gmix])
        dve(lambda e: e.tensor_scalar(out=gffn[:], in0=gffn[:], scalar1=32.0, scalar2=None, op0=ALU.mult), [gffn], [gffn])
        dve(lambda e: e.tensor_scalar(out=pfh[:], in0=pfm[:], scalar1=0.5, scalar2=None, op0=ALU.mult), [pfm], [pfh])
        act(lambda e: e.activation(out=Aneg[:], in_=prow[:, 4:8], func=AF.Exp), [prow], [Aneg])
        dve(lambda e: e.tensor_scalar(out=Aneg[:], in0=Aneg[:], scalar1=-1.0, scalar2=None, op0=ALU.mult), [Aneg], [Aneg])
        act(lambda e: e.activation(out=esink[:], in_=prow[:, 12:16], func=AF.Exp), [prow], [esink])
        dve(lambda e: e.tensor_scalar(out=g16[:], in0=prow[:, 16:272], scalar1=16.0, scalar2=None, op0=ALU.mult), [prow], [g16])

    diagB = Buf("diag")

    WIN_GROUPS = [(256, 768), (768, 1284), (1284, 2052), (2052, 2564), (0, 256)]
    WG = {}

    def wbufs(c0, c1):
        out = []
        for gi, (a, b) in enumerate(WIN_GROUPS):
            if c0 < b and c1 > a:
                out.extend(WG[gi])
        return out

    FFN_GROUPS = [(0, 6), (6, 12), (12, 17), (17, 22)]
    WF = {}

    def load_mixer_weights(l):
        del WA[:]
        del WB[:]
        WB.append(diagB)
        w_in_v = w_in[l].rearrange("(k p) n -> p k n", p=128)
        for gi, (c0, c1) in enumerate(WIN_GROUPS):
            WG[gi] = []
            wdma(w_in_sb[:, :, c0:c1], w_in_v[:, :, c0:c1], WG[gi])
            WA.extend(WG[gi])
        for i, r0 in enumerate((0, 128, 512, 640, 768, 896)):
            wdma(w_out_m[:, i, :], w_out[l, r0:r0 + 128, :], WB)
        for h in range(4):
            wdma(w_out_b[:, h, :], w_out[l, 256 + 64 * h:320 + 64 * h, :], WB)
        for i_ in range(22):
            col = i_ if i_ < 16 else 20 + (i_ - 16)
            dve(lambda e, i_=i_, col=col: e.tensor_scalar(out=dgx[:, i_, :], in0=ident[:], scalar1=pfm[:, col:col + 1], scalar2=None, op0=ALU.mult),
                [ident, pfm], [dgx])
        dma("pool", cbrow[:], ssm_conv_b_d[l:l + 1, :], [], [cbrow])
        for c in range(2):
            for k in range(31):
                col = 26 + c * 31 + k
                dve(lambda e, c=c, k=k, col=col: e.tensor_scalar(out=diag[:, c * 31 + k, :], in0=ident[:], scalar1=pfm[:, col:col + 1],
                                                                 scalar2=None, op0=ALU.mult), [ident, pfm], [diagB])

    def load_ffn_weights(l):
        del WA[:]
        del WB[:]
        wg_v = w_gate[l].rearrange("(k p) n -> p k n", p=128)
        wu_v = w_up[l].rearrange("(k p) n -> p k n", p=128)
        for gi, (j0, j1) in enumerate(FFN_GROUPS):
            WF[gi] = []
            wdma(wg_sb[:, :, j0 * 128:j1 * 128], wg_v[:, :, j0 * 128:j1 * 128], WF[gi])
            wdma(wu_sb[:, :, j0 * 128:j1 * 128], wu_v[:, :, j0 * 128:j1 * 128], WF[gi])
            WA.extend(WF[gi])
        for j in range(NFF):
            wdma(wd_sb[:, j, :], w_down[l, j * 128:(j + 1) * 128, :], WB)

    def init_prompt_ctx():
        c = ctxP
        dve(lambda e: e.memset(c.hT[:], 0.0), [], [c.hT])
        dve(lambda e: e.memset(c.hTb[:], 0.0), [], [c.hTb])
        dve(lambda e: e.memset(c.tx[:], 0.0), [], [c.tx])
        dve(lambda e: e.memset(c.tsc[:], 0.0), [], [c.tsc])
        dve(lambda e: e.memset(c.cbc[:], 0.0), [], [c.cbc])
        c.has_prev = False
        c.par = 0

    def init_sample_ctx(l, s):
        c = ctxS
        c.par = 0
        c.has_prev = True
        for hh in range(2):
            ldt = ldts[_rot[1] % 2]; _rot[1] += 1
            dma("sp", ldt[:, 0:128], st_ssm[l, s, hh * 128:(hh + 1) * 128, :], [], [ldt])
            transpose_to(c.hT[:, hh * 128:(hh + 1) * 128], c.hT, ldt[:, 0:128], ldt, 128, 128, banks[0])
        act(lambda e: e.activation(out=c.hTb[:], in_=c.hT[:], func=AF.Copy), [c.hT], [c.hTb])
        ldt = ldts[_rot[1] % 2]; _rot[1] += 1
        dma("sp", ldt[0:3, 0:512], st_xc[l, s], [], [ldt])
        for cc in range(4):
            transpose_to(c.tx[:, cc, :], c.tx, ldt[0:3, cc * 128:(cc + 1) * 128], ldt, 3, 128, banks[0])
        ldt = ldts[_rot[1] % 2]; _rot[1] += 1
        dma("sp", ldt[0:2, 0:256], st_sc[l, s], [], [ldt])
        for cc in range(2):
            transpose_to(c.tsc[:, cc, :], c.tsc, ldt[0:2, cc * 128:(cc + 1) * 128], ldt, 2, 128, banks[0])
        ldt = ldts[_rot[1] % 2]; _rot[1] += 1
        dma("sp", ldt[0:30, 0:256], st_cc[l, s], [], [ldt])
        for cc in range(2):
            transpose_to(c.cbc[:, cc, 0:30], c.cbc, ldt[0:30, cc * 128:(cc + 1) * 128], ldt, 30, 128, banks[0])
        ldt = ldts[_rot[1] % 2]; _rot[1] += 1
        dma("sp", ldt[:, 0:128], ck[l, s], [], [ldt])
        for g in range(2):
            transpose_to(c.kT[1][:, g, :], c.kT[1], ldt[:, g * 64:(g + 1) * 64], ldt, 128, 64, banks[0])
        ldt = ldts[_rot[1] % 2]; _rot[1] += 1
        dma("sp", ldt[:, 0:128], cv[l, s], [], [ldt])
        act(lambda e: e.activation(out=c.v[1][:].rearrange("p a b -> p (a b)"), in_=ldt[:, 0:128], func=AF.Copy), [ldt], [c.v[1]])

    import os

    def out_rows_T(dst_dram, src_ap, src_tt, R, C):
        pv = banks[0].t[0:R, 0:C]
        KORT = int(os.environ.get("KORT", "0"))
        if KORT != 2:
            pe(lambda e: e.transpose(pv, src_ap, ident[0:C, 0:C]), [src_tt, ident], [banks[0]])
            otr = otrs[_rot[0] % 4]; _rot[0] += 1
            dve(lambda e: e.tensor_copy(out=otr[0:R, 0:C], in_=pv), [banks[0]], [otr])
        if KORT != 1:
            dma("sp", dst_dram, otr[0:R, 0:C], [otr], [])

    def mixer_head_norm(L, x_src, x_src_b, ti):
        xt = x_t[ti % 2]; xn = xn2[ti % 2]
        dma("sp", xt[0:L, :], x_src, [x_src_b] if x_src_b is not None else [], [xt])
        rmsnorm_tile(xt, xt[0:L, :], L, gmix, xn[0:L, :], xn, 0)

    def mixer_head_tr(L, ti):
        xn = xn2[ti % 2]; xnT = xnT2[ti % 2]
        T0 = banks[0]
        T0v = bf(T0)
        def tr8(e):
            for k in range(8):
                ins = e.transpose(T0v[:, k * 128:k * 128 + L], xn[0:L, k * 128:(k + 1) * 128], identb[0:L, 0:L])
            return ins
        pe(tr8, [xn, identb], [T0])
        xnT_v = xnT[:, :, 0:L]
        act(lambda e: e.activation(out=xnT_v, in_=T0v.rearrange("p (k t) -> p k t", k=8)[:, :, 0:L], func=AF.Copy), [T0], [xnT])

    def mixer_front_A(c, L, ti):
        xnT = xnT2[ti % 2]
        bA, bK = banks[1], banks[3]
        def dt_group(e):
            for k in range(8):
                ins = e.matmul(bK.t[0:L, 408:412], lhsT=xnT[:, k, 0:L], rhs=w_in_sb[:, k, 768:772], start=(k == 0), stop=(k == 7))
            return ins
        pe(dt_group, [wbufs(768, 772), xnT], [bK])
        def f(e):
            for i, c0 in enumerate([256, 384, 512, 640]):
                for k in range(8):
                    ins = e.matmul(bA.t[0:128, i * L:(i + 1) * L], lhsT=w_in_sb[:, k, c0:c0 + 128], rhs=xnT[:, k, 0:L], start=(k == 0), stop=(k == 7))
            return ins
        pe(f, [wbufs(256, 768), xnT], [bA])
        dve(lambda e: e.tensor_copy(out=cbx[:, :, 0:3], in_=c.tx[:]), [c.tx], [cbx])
        act(lambda e: e.activation(out=cbx[:, :, 3:3 + L], in_=bA.t[:, 0:4 * L].rearrange("p (c t) -> p c t", c=4), func=AF.Copy),
            [bA], [cbx])
        act(lambda e: e.activation(out=cbxb[:, :, 0:3 + L], in_=cbx[:, :, 0:3 + L], func=AF.Copy), [cbx], [cbxb])
        dve(lambda e: e.tensor_copy(out=c.tx[:], in_=cbx[:, :, L:L + 3]), [cbx], [c.tx])
        dve(lambda e: e.tensor_tensor(out=dtr[0:L, :], in0=bK.t[0:L, 408:412], in1=prow[0:L, 0:4], op=ALU.add), [bK, prow], [dtr])

    def mixer_tile(l, c, L, x_src, x_src_b, x_dst, x_dst_b, kind, s, is_last, ti, pre_chain=None, pre_in=None, skip_A=False, next_A=None):
        if pre_in is not None:
            pre_in()
        xt = x_t[ti % 2]; xnT = xnT2[ti % 2]
        T0 = banks[0]
        T0v = bf(T0)
        cutt(3, ti)
        def fm_group(bank, ncol, M, cols):
            def f(e):
                for i, c0 in enumerate(cols):
                    for k in range(8):
                        ins = e.matmul(bank.t[0:M, i * L:(i + 1) * L], lhsT=w_in_sb[:, k, c0:c0 + M], rhs=xnT[:, k, 0:L],
                                       start=(k == 0), stop=(k == 7))
                return ins
            pe(f, [wbufs(min(cols), max(cols) + M), xnT], [bank])
        bA, bQ, bK, bS1, bS2, bC, bZ = banks[1], banks[2], banks[3], banks[4], banks[5], banks[6], banks[7]
        if not skip_A:
            mixer_front_A(c, L, ti)
        fm_group(bQ, 4, 64, [772, 836, 900, 964])
        fm_group(bK, 2, 64, [1028, 1092])
        def tok_group(e):
            for k in range(8):
                e.matmul(bZ.t[0:L, 0:256], lhsT=xnT[:, k, 0:L], rhs=w_in_sb[:, k, 0:256], start=(k == 0), stop=(k == 7))
            for k in range(8):
                ins = e.matmul(bZ.t[0:L, 256:512], lhsT=xnT[:, k, 0:L], rhs=w_in_sb[:, k, 1028:1284], start=(k == 0), stop=(k == 7))
            return ins
        pe(tok_group, [wbufs(0, 256), wbufs(1028, 1284), xnT], [bZ])
        fm_group(bS1, 4, 128, [1284, 1412, 1540, 1668])
        fm_group(bS2, 2, 128, [1796, 1924])
        fm_group(bC, 4, 128, [2052, 2180, 2308, 2436])
        cutt(4, ti)
        cutn(1, ti)
        cutn(2, ti)
        cutn(3, ti)
        cutn(4, ti)
        cutn(5, ti)
        act(lambda e: e.activation(out=qT[:, :, 0:L], in_=bQ.t[0:64, 0:4 * L].rearrange("p (c t) -> p c t", c=4), func=AF.Copy), [bQ], [qT])
        cutn(6, ti)
        kTc = c.kT[c.par]; vc = c.v[c.par]; kTp = c.kT[1 - c.par]; vp = c.v[1 - c.par]
        cutn(7, ti)
        act(lambda e: e.activation(out=kTc[:, :, 0:L], in_=bK.t[0:64, 0:2 * L].rearrange("p (c t) -> p c t", c=2), func=AF.Copy), [bK], [kTc])
        cutn(8, ti)
        dve(lambda e: e.tensor_copy(out=vc[0:L, :, :].rearrange("p a b -> p (a b)"), in_=bZ.t[0:L, 384:512]), [bZ], [vc])
        cutn(9, ti)
        if is_last or kind == "s":
            KVAR = int(os.environ.get("KVAR", "4"))
            if KVAR == 4:
                dve(lambda e: e.tensor_copy(out=kv32[0:L, :], in_=bZ.t[0:L, 256:512]), [bZ], [kv32])
            elif KVAR != 2:
                act(lambda e: e.activation(out=kv32[0:L, :], in_=bZ.t[0:L, 256:512], func=AF.Copy), [bZ], [kv32])
            if kind == "p" and KVAR != 1:
                dma("sp", o_k_p[l], kv32[:, 0:128], [kv32], [])
                if KVAR != 3:
                    dma("sp", o_v_p[l], kv32[:, 128:256], [kv32], [])
            elif kind == "p":
                pass
            else:
                dma("sp", o_k_s[l, s], kv32[0:L, 0:128], [kv32], [])
                dma("sp", o_v_s[l, s], kv32[0:L, 128:256], [kv32], [])
        cutn(10, ti)
        sigm(tz[0:L, :], tz, bZ.t[0:L, 0:256], bZ, L)
        cutn(11, ti)
        dve(lambda e: e.tensor_tensor(out=g1[0:L, :], in0=tz[0:L, :], in1=bZ.t[0:L, 0:256], op=ALU.mult), [tz, bZ], [g1])
        cutn(12, ti)
        act(lambda e: e.activation(out=bg_sb[:, :, 0:L], in_=bS1.t[:, 0:2 * L].rearrange("p (c t) -> p c t", c=2), func=AF.Copy), [bS1], [bg_sb])
        cutn(13, ti)
        act(lambda e: e.activation(out=h_sb[:, :, 0:L], in_=bS2.t[:, 0:2 * L].rearrange("p (c t) -> p c t", c=2), func=AF.Copy), [bS2], [h_sb])
        cutn(14, ti)
        dve(lambda e: e.tensor_copy(out=cbs[:, :, 0:2], in_=c.tsc[:]), [c.tsc], [cbs])
        cutn(15, ti)
        dve(lambda e: e.tensor_tensor(out=cbs[:, :, 2:2 + L], in0=bS1.t[:, 2 * L:4 * L].rearrange("p (c t) -> p c t", c=2),
                                      in1=h_sb[:, :, 0:L], op=ALU.mult), [bS1, h_sb], [cbs])
        act(lambda e: e.activation(out=cbsb[:, :, 0:2 + L], in_=cbs[:, :, 0:2 + L], func=AF.Copy), [cbs], [cbsb])
        cutn(16, ti)
        dve(lambda e: e.tensor_copy(out=c.tsc[:], in_=cbs[:, :, L:L + 2]), [cbs], [c.tsc])
        cutn(17, ti)
        sigm(tg[:, :, 0:L], tg, bC.t[:, 2 * L:4 * L].rearrange("p (c t) -> p c t", c=2), bC, 128)
        cutn(18, ti)
        dve(lambda e: e.tensor_tensor(out=glu32[:, :, 0:L], in0=tg[:, :, 0:L], in1=bC.t[:, 0:2 * L].rearrange("p (c t) -> p c t", c=2), op=ALU.mult),
            [tg, bC], [glu32])
        cutn(19, ti)
        cutn(20, ti)
        act(lambda e: e.activation(out=c.cbc[:, :, 30:30 + L], in_=glu32[:, :, 0:L], func=AF.Copy), [glu32], [c.cbc])

        cutt(5, ti)
        KOUT = int(os.environ.get("KOUT", "0"))
        if is_last or kind == "s":
            for cc in range(4 if KOUT in (0, 1) else 0):
                dst = (o_xc_p[l, :, cc * 128:(cc + 1) * 128] if kind == "p" else o_xc_s[l, s, :, cc * 128:(cc + 1) * 128])
                out_rows_T(dst, c.tx[:, cc, :], c.tx, 3, 128)
            for cc in range(2 if KOUT in (0, 2) else 0):
                dst = (o_sc_p[l, :, cc * 128:(cc + 1) * 128] if kind == "p" else o_sc_s[l, s, :, cc * 128:(cc + 1) * 128])
                out_rows_T(dst, c.tsc[:, cc, :], c.tsc, 2, 128)
            for cc in range(2 if KOUT in (0, 3) else 0):
                if kind == "p":
                    out_rows_T(o_cc_p[l, :, cc * 128:(cc + 1) * 128], glu32[:, cc, L - 30:L], glu32, 30, 128)
                else:
                    out_rows_T(o_cc_s[l, s, 14:30, cc * 128:(cc + 1) * 128], glu32[:, cc, 0:16], glu32, 16, 128)
            if kind == "s":
                dma("sp", o_cc_s[l, s, 0:14, :], st_cc[l, s, 16:30, :], [], [])

        cutt(6, ti)
        def gen_dt():
            act(lambda e: e.activation(out=dtv[0:L, :], in_=dtr[0:L, :], func=AF.Exp), [dtr], [dtv])
            yield
            act(lambda e: e.activation(out=dtv[0:L, :], in_=dtv[0:L, :], func=AF.Ln, bias=onesf[0:L, 0:1]), [dtv, onesf], [dtv])
            yield
            dve(lambda e: e.tensor_tensor(out=av[0:L, :], in0=dtv[0:L, :], in1=Aneg[0:L, :], op=ALU.mult), [dtv, Aneg], [av])
            yield
            dve(lambda e: e.tensor_tensor(out=aU[0:L, :, 0:L], in0=utri[0:L, 0:L].unsqueeze(1).to_broadcast([L, 4, L]),
                                          in1=av[0:L, :].unsqueeze(2).to_broadcast([L, 4, L]), op=ALU.mult), [utri, av], [aU])
            yield
            dve(lambda e: e.tensor_scalar(out=na[0:L, :, 0:L], in0=av[0:L, :].unsqueeze(2).to_broadcast([L, 4, L]), scalar1=-1.0, scalar2=None,
                                          op0=ALU.mult), [av], [na])
            yield
            def segmm(e):
                for h in range(4):
                    o = bA.t[0:L, h * L:(h + 1) * L]
                    e.matmul(o, lhsT=onesf[0:L, 0:L], rhs=aU[0:L, h, 0:L], start=True, stop=False)
                    e.matmul(o, lhsT=utri[0:L, 0:L], rhs=na[0:L, h, 0:L], start=False, stop=False)
                    ins = e.matmul(o, lhsT=ident[0:L, 0:L], rhs=maskb4[0:L, h, 0:L], start=False, stop=True)
                return ins
            pe(segmm, [onesf, utri, ident, maskb4, aU, na], [bA])
            yield
            act(lambda e: e.activation(out=decay[0:L, :, 0:L], in_=bA.t[0:L, 0:4 * L].rearrange("p (h t) -> p h t", h=4), func=AF.Exp), [bA], [decay])
            yield
            def smallmm(e):
                e.matmul(bK.t[0:L, 392:396], lhsT=utri[0:L, 0:L], rhs=av[0:L, :], start=True, stop=True)
                return e.matmul(bK.t[:, 400:404], lhsT=onesf[0:L, :], rhs=av[0:L, :], start=True, stop=True)
            pe(smallmm, [utri, onesf, av], [bK])
            yield
            act(lambda e: e.activation(out=eac[0:L, :], in_=bK.t[0:L, 392:396], func=AF.Exp), [bK], [eac])
            yield
            act(lambda e: e.activation(out=bd[:], in_=bK.t[:, 400:404], func=AF.Exp), [bK], [bd])
            yield

        def gen_state():
            dve(lambda e: e.tensor_tensor(out=xdte[0:L, :].rearrange("p (h d) -> p h d", h=4), in0=xdt[0:L, :].rearrange("p (h d) -> p h d", h=4),
                                          in1=decay[0:L, :, L - 1:L].to_broadcast([L, 4, 64]), op=ALU.mult), [xdt, decay], [xdte])
            yield
            pe(lambda e: e.matmul(bZ.t[:, 256:512], lhsT=B_tok[0:L, :], rhs=xdte[0:L, :], start=True, stop=True), [B_tok, xdte], [bZ])
            yield
            dve(lambda e: e.tensor_tensor(out=c.hT[:].rearrange("p (h d) -> p h d", h=4), in0=c.hT[:].rearrange("p (h d) -> p h d", h=4),
                                          in1=bd[:].unsqueeze(2).to_broadcast([128, 4, 64]), op=ALU.mult), [c.hT, bd], [c.hT])
            yield
            dve(lambda e: e.tensor_tensor(out=c.hT[:], in0=c.hT[:], in1=bZ.t[:, 256:512], op=ALU.add), [c.hT, bZ], [c.hT])
            yield
            act(lambda e: e.activation(out=c.hTb[:], in_=c.hT[:], func=AF.Copy), [c.hT], [c.hTb])
            yield
            if is_last or kind == "s":
                for hh in range(2):
                    dstd = (o_ssm_p[l, hh * 128:(hh + 1) * 128, :] if kind == "p" else o_ssm_s[l, s, hh * 128:(hh + 1) * 128, :])
                    out_rows_T(dstd, c.hT[:, hh * 128:(hh + 1) * 128], c.hT, 128, 128)
                    yield


        def gen_ssd():
            def xconv(e):
                for cc in range(4):
                    o = bS2.t[:, cc * L:(cc + 1) * L]
                    for k in range(4):
                        e.matmul(o, lhsT=dgx[:, cc * 4 + k, :], rhs=cbxb[:, cc, k:k + L], start=(k == 0), stop=False)
                    ins = e.matmul(o, lhsT=cbrow[0:1, cc * 128:(cc + 1) * 128], rhs=onesb[0:1, 0:L], start=False, stop=True)
                return ins
            pe(xconv, [dgx, cbxb, cbrow, onesb], [bS2])
            yield
            accv = bS2.t[:, 0:4 * L].rearrange("p (c t) -> p c t", c=4)
            act(lambda e: e.activation(out=tnx[:, :, 0:L], in_=accv, func=AF.Exp, scale=-1.0), [bS2], [tnx])
            yield
            act(lambda e: e.activation(out=tnx[:, :, 0:L], in_=tnx[:, :, 0:L], func=AF.Ln, bias=onesf[:, 0:1]), [tnx, onesf], [tnx])
            yield
            act(lambda e: e.activation(out=tnx[:, :, 0:L], in_=tnx[:, :, 0:L], func=AF.Exp, scale=-1.0), [tnx], [tnx])
            yield
            dve(lambda e: e.tensor_tensor(out=xbcs[:, :, 0:L], in0=tnx[:, :, 0:L], in1=accv, op=ALU.mult), [tnx, bS2], [xbcs])
            yield
            def trx(e):
                e.transpose(T0v[0:L, 0:128], xbcs[:, 0, 0:L], identb[:])
                e.transpose(T0v[0:L, 128:256], xbcs[:, 1, 0:L], identb[:])
                return e.transpose(T0v[0:L, 256:384], xbcs[:, 2, 0:L], identb[:])
            pe(trx, [xbcs, identb], [T0])
            yield
            act(lambda e: e.activation(out=xs_tok[0:L, :], in_=T0v[0:L, 0:256], func=AF.Copy), [T0], [xs_tok])
            yield
            act(lambda e: e.activation(out=B_tok[0:L, :], in_=T0v[0:L, 256:384], func=AF.Copy), [T0], [B_tok])
            yield
            dve(lambda e: e.tensor_tensor(out=xdt[0:L, :].rearrange("p (h d) -> p h d", h=4), in0=xs_tok[0:L, :].rearrange("p (h d) -> p h d", h=4),
                                          in1=dtv[0:L, :].unsqueeze(2).to_broadcast([L, 4, 64]), op=ALU.mult), [xs_tok, dtv], [xdt])
            yield
            dve(lambda e: e.tensor_tensor(out=y3[0:L, :].rearrange("p (h d) -> p h d", h=4), in0=xs_tok[0:L, :].rearrange("p (h d) -> p h d", h=4),
                                          in1=prow[0:L, 8:12].unsqueeze(2).to_broadcast([L, 4, 64]), op=ALU.mult), [xs_tok, prow], [y3])
            yield
            dve(lambda e: e.tensor_tensor(out=y3[0:L, :], in0=y3[0:L, :], in1=g1[0:L, :], op=ALU.mult), [y3, g1], [y3])
            yield
            pe(lambda e: e.matmul(bK.t[0:L, 0:L], lhsT=xbcs[:, 2, 0:L], rhs=xbcs[:, 3, 0:L], start=True, stop=True), [xbcs], [bK])
            yield
            dve(lambda e: e.tensor_tensor(out=MT[0:L, :, 0:L], in0=decay[0:L, :, 0:L], in1=bK.t[0:L, 0:L].unsqueeze(1).to_broadcast([L, 4, L]),
                                          op=ALU.mult), [decay, bK], [MT])
            yield
            def yintra(e):
                for h in range(4):
                    ins = e.matmul(bS1.t[0:L, h * 64:(h + 1) * 64], lhsT=MT[0:L, h, 0:L], rhs=xdt[0:L, h * 64:(h + 1) * 64], start=True, stop=True)
                return ins
            pe(yintra, [MT, xdt], [bS1])
            yield
            pe(lambda e: e.matmul(bS2.t[0:L, 0:256], lhsT=xbcs[:, 3, 0:L], rhs=c.hTb[:], start=True, stop=True), [xbcs, c.hTb], [bS2])
            yield
            gens.append(gen_state())
            dve(lambda e: e.tensor_tensor(out=y1[0:L, :].rearrange("p (h d) -> p h d", h=4), in0=bS2.t[0:L, 0:256].rearrange("p (h d) -> p h d", h=4),
                                          in1=eac[0:L, :].unsqueeze(2).to_broadcast([L, 4, 64]), op=ALU.mult), [bS2, eac], [y1])
            yield
            dve(lambda e: e.tensor_tensor(out=y1[0:L, :], in0=y1[0:L, :], in1=bS1.t[0:L, 0:256], op=ALU.add), [y1, bS1], [y1])
            yield
            dve(lambda e: e.tensor_tensor(out=y1[0:L, :], in0=y1[0:L, :], in1=g1[0:L, :], op=ALU.mult), [y1, g1], [y1])
            yield
            dve(lambda e: e.tensor_tensor(out=y1[0:L, :], in0=y1[0:L, :], in1=y3[0:L, :], op=ALU.add), [y1, y3], [y1])
            yield
            dve(lambda e: e.memset(ss[0:L, 1:2], 0.0), [], [ss])
            yield
            act(lambda e: e.activation(out=junk[0:L, 0:256], in_=y1[0:L, :], func=AF.Square, accum_out=ss[0:L, 1:2]), [y1, ss], [junk, ss])
            yield
            rsqrt_col(rs[0:L, 1:2], ss[0:L, 1:2], (rs, ss), 1.0, 256.0 * EPS)
            yield
            dve(lambda e: e.scalar_tensor_tensor(out=ynb[0:L, :], in0=y1[0:L, :], scalar=rs[0:L, 1:2], in1=g16[0:L, :], op0=ALU.mult, op1=ALU.mult),
                [y1, rs, g16], [ynb])
            yield
            def tra(e):
                e.transpose(T0v[:, 512:512 + L], ynb[0:L, 0:128], identb[0:L, 0:L])
                return e.transpose(T0v[:, 640:640 + L], ynb[0:L, 128:256], identb[0:L, 0:L])
            pe(tra, [ynb, identb], [T0])
            yield
            act(lambda e: e.activation(out=mixa[:, :, 0:L], in_=T0v[:, 512:768].rearrange("p (c t) -> p c t", c=2)[:, :, 0:L], func=AF.Copy), [T0], [mixa])
            yield

        def gen_cc():
            def sconvmm(e):
                for cc in range(2):
                    for k in range(3):
                        ins = e.matmul(bS1.t[:, 256 + cc * L:256 + (cc + 1) * L], lhsT=dgx[:, 16 + cc * 3 + k, :], rhs=cbsb[:, cc, k:k + L],
                                       start=(k == 0), stop=(k == 2))
                return ins
            pe(sconvmm, [dgx, cbsb], [bS1])
            yield
            dve(lambda e: e.tensor_tensor(out=mixc[:, :, 0:L], in0=bg_sb[:, :, 0:L],
                                          in1=bS1.t[:, 256:256 + 2 * L].rearrange("p (c t) -> p c t", c=2), op=ALU.mult), [bg_sb, bS1], [mixc])
            yield

            def ccmm(e):
                for cc in range(2):
                    for k in range(31):
                        ins = e.matmul(bZ.t[:, cc * L:(cc + 1) * L], lhsT=diag[:, cc * 31 + k, :], rhs=c.cbc[:, cc, k:k + L], start=(k == 0), stop=(k == 30))
                return ins
            pe(ccmm, [WB, c.cbc], [bZ])
            yield
            if kind == "p":
                dve(lambda e: e.tensor_copy(out=c.cbc[:, :, 0:30], in_=c.cbc[:, :, L:L + 30]), [c.cbc], [c.cbc])
                yield
            for cc in range(2):
                act(lambda e, cc=cc: e.activation(out=ycs[:, cc, 0:L], in_=bZ.t[:, cc * L:(cc + 1) * L], func=AF.Identity, bias=pfm[:, 88 + cc:89 + cc]),
                    [bZ, pfm], [ycs])
                yield
            act(lambda e: e.activation(out=ysq[:, :, 0:L], in_=ycs[:, :, 0:L], func=AF.Square), [ycs], [ysq])
            yield
            def lnmm(e):
                e.matmul(bK.t[:, 128:128 + L], lhsT=ones256[:], rhs=ycs[:, 0, 0:L], start=True, stop=False)
                e.matmul(bK.t[:, 128:128 + L], lhsT=ones256[:], rhs=ycs[:, 1, 0:L], start=False, stop=True)
                e.matmul(bK.t[:, 256:256 + L], lhsT=ones256[:], rhs=ysq[:, 0, 0:L], start=True, stop=False)
                return e.matmul(bK.t[:, 256:256 + L], lhsT=ones256[:], rhs=ysq[:, 1, 0:L], start=False, stop=True)
            pe(lnmm, [ones256, ycs, ysq], [bK])
            yield
            act(lambda e: e.activation(out=lnv[:, 0:L], in_=bK.t[:, 128:128 + L], func=AF.Square), [bK], [lnv])
            yield
            dve(lambda e: e.tensor_tensor(out=lnv[:, 0:L], in0=bK.t[:, 256:256 + L], in1=lnv[:, 0:L], op=ALU.subtract), [bK, lnv], [lnv])
            yield
            act(lambda e: e.activation(out=lnr[:, 0:L], in_=lnv[:, 0:L], func=AF.Ln, bias=epsc[:, 2:3]), [lnv, epsc], [lnr])
            yield
            act(lambda e: e.activation(out=lnr[:, 0:L], in_=lnr[:, 0:L], func=AF.Exp, scale=-0.5), [lnr], [lnr])
            yield
            for cc in range(2):
                dve(lambda e, cc=cc: e.tensor_tensor(out=lnd[:, cc, 0:L], in0=ycs[:, cc, 0:L], in1=bK.t[:, 128:128 + L], op=ALU.subtract), [ycs, bK], [lnd])
                yield
                dve(lambda e, cc=cc: e.tensor_tensor(out=lnd[:, cc, 0:L], in0=lnd[:, cc, 0:L], in1=lnr[:, 0:L], op=ALU.mult), [lnd, lnr], [lnd])
                yield
                dve(lambda e, cc=cc: e.tensor_scalar(out=lnd[:, cc, 0:L], in0=lnd[:, cc, 0:L], scalar1=pfm[:, 90 + cc:91 + cc], scalar2=pfm[:, 92 + cc:93 + cc],
                                                     op0=ALU.mult, op1=ALU.add), [lnd, pfm], [lnd])
                yield
            act(lambda e: e.activation(out=tl[:, :, 0:L], in_=lnd[:, :, 0:L], func=AF.Exp, scale=-1.0), [lnd], [tl])
            yield
            act(lambda e: e.activation(out=tl[:, :, 0:L], in_=tl[:, :, 0:L], func=AF.Ln, bias=onesf[:, 0:1]), [tl, onesf], [tl])
            yield
            act(lambda e: e.activation(out=tl[:, :, 0:L], in_=tl[:, :, 0:L], func=AF.Exp, scale=-1.0), [tl], [tl])
            yield
            dve(lambda e: e.tensor_tensor(out=mixd[:, :, 0:L], in0=tl[:, :, 0:L], in1=lnd[:, :, 0:L], op=ALU.mult), [tl, lnd], [mixd])
            yield


        def gen_swa():
            blocks = []
            if kind == "p":
                if c.has_prev:
                    blocks.append((kTp, vp, 128, biasP[:, :, :]))
                blocks.append((kTc, vc, 128, biasC[:, :, :]))
            else:
                blocks.append((kTp, vp, 128, biasSc[:, :, :]))
                blocks.append((kTc, vc, L, biasSn[:, :, :]))
            sb_banks = [bQ, bC]
            for bi, (kT_b, v_b, Lk, bias_ap) in enumerate(blocks):
                bk = sb_banks[bi]
                def smm(e, kT_b=kT_b, Lk=Lk, bk=bk):
                    for h in range(4):
                        ins = e.matmul(bk.t[0:Lk, h * L:(h + 1) * L], lhsT=kT_b[:, h // 2, 0:Lk], rhs=qT[:, h, 0:L], start=True, stop=True)
                    return ins
                pe(smm, [kT_b, qT], [bk])
                yield
                st_ = stmp[bi]; E_ = Eb[bi]
                dve(lambda e, bk=bk, Lk=Lk, st_=st_, bias_ap=bias_ap: e.scalar_tensor_tensor(
                    out=st_[0:Lk, :, 0:L], in0=bk.t[0:Lk, 0:4 * L].rearrange("p (h t) -> p h t", h=4), scalar=0.125,
                    in1=bias_ap[0:Lk, :, 0:L], op0=ALU.mult, op1=ALU.add), [bk, biasP, biasC, biasSc, biasSn], [st_])
                yield
                act(lambda e, st_=st_, E_=E_, Lk=Lk: e.activation(out=E_[0:Lk, :, 0:L], in_=st_[0:Lk, :, 0:L], func=AF.Exp), [st_], [E_])
                yield
            nb = len(blocks)
            def pvmm(e):
                for h in range(4):
                    for bi, (kT_b, v_b, Lk, _) in enumerate(blocks):
                        e.matmul(bQ.t[0:64, h * L:(h + 1) * L], lhsT=v_b[0:Lk, h // 2, :], rhs=Eb[bi][0:Lk, h, 0:L], start=(bi == 0), stop=(bi == nb - 1))
                for h in range(4):
                    for bi, (kT_b, v_b, Lk, _) in enumerate(blocks):
                        ins = e.matmul(bC.t[0:64, h * L:(h + 1) * L], lhsT=onesb[0:Lk, 0:64], rhs=Eb[bi][0:Lk, h, 0:L], start=(bi == 0), stop=(bi == nb - 1))
                return ins
            pe(pvmm, [b[1] for b in blocks] + [Eb[0], Eb[1], onesb], [bQ, bC])
            yield
            dve(lambda e: e.tensor_tensor(out=rden[:, :, 0:L], in0=bC.t[0:64, 0:4 * L].rearrange("p (h t) -> p h t", h=4),
                                          in1=esink[0:64, :].unsqueeze(2).to_broadcast([64, 4, L]), op=ALU.add), [bC, esink], [rden])
            yield
            act(lambda e: e.activation(out=rden[:, :, 0:L], in_=rden[:, :, 0:L], func=AF.Ln), [rden], [rden])
            yield
            act(lambda e: e.activation(out=rden[:, :, 0:L], in_=rden[:, :, 0:L], func=AF.Exp, scale=-1.0), [rden], [rden])
            yield
            dve(lambda e: e.tensor_tensor(out=mixb[:, :, 0:L], in0=bQ.t[0:64, 0:4 * L].rearrange("p (h t) -> p h t", h=4), in1=rden[:, :, 0:L], op=ALU.mult),
                [bQ, rden], [mixb])
            yield


        if pre_chain is not None:
            pre_chain()
        gens = [gen_dt(), gen_ssd(), gen_swa(), gen_cc()]
        while gens:
            for g_ in list(gens):
                try:
                    next(g_)
                except StopIteration:
                    gens.remove(g_)
        if next_A is not None:
            next_A()
        cutt(10, ti)
        def opmm(e):
            for half in range(2):
                bk = (banks[4], banks[5])[half]
                o = bk.t[0:L, :]
                cs = slice(half * 512, (half + 1) * 512)
                e.matmul(o, lhsT=mixa[:, 0, 0:L], rhs=w_out_m[:, 0, cs], start=True, stop=False)
                e.matmul(o, lhsT=mixa[:, 1, 0:L], rhs=w_out_m[:, 1, cs], start=False, stop=False)
                for h in range(4):
                    e.matmul(o, lhsT=mixb[:, h, 0:L], rhs=w_out_b[:, h, cs], start=False, stop=False)
                e.matmul(o, lhsT=mixc[:, 0, 0:L], rhs=w_out_m[:, 2, cs], start=False, stop=False)
                e.matmul(o, lhsT=mixc[:, 1, 0:L], rhs=w_out_m[:, 3, cs], start=False, stop=False)
                e.matmul(o, lhsT=mixd[:, 0, 0:L], rhs=w_out_m[:, 4, cs], start=False, stop=False)
                ins = e.matmul(o, lhsT=mixd[:, 1, 0:L], rhs=w_out_m[:, 5, cs], start=False, stop=True)
            return ins
        pe(opmm, [mixa, mixb, mixc, mixd, WB], [banks[4], banks[5]])
        dve(lambda e: e.tensor_tensor(out=xt[0:L, 0:512], in0=xt[0:L, 0:512], in1=banks[4].t[0:L, :], op=ALU.add), [xt, banks[4]], [xt])
        dve(lambda e: e.tensor_tensor(out=xt[0:L, 512:1024], in0=xt[0:L, 512:1024], in1=banks[5].t[0:L, :], op=ALU.add), [xt, banks[5]], [xt])
        dma("sp", x_dst, xt[0:L, :], [xt], [x_dst_b])
        c.par = 1 - c.par
        c.has_prev = True
        cutt(11, ti)

    bg_sb = sbm([128, 2, 128], F32, "bg_sb")

    FT = 256
    xf2 = [sbf([128, FT // 128, D], F32, "xf") for _ in range(2)]
    hn = sbf([128, D], BF16, "hn")
    hfT = sbf([128, 8, FT], BF16, "hfT")
    hhT = sbf([128, NFF, FT], BF16, "hhT")
    tgt = [sbf([128, FT], F32, "tgt") for _ in range(2)]
    yo = [sbf([128, D], F32, "yo") for _ in range(2)]

    def ffn_head(it):
        l, nsub, Ls, srcs, dsts, final, xf = it
        T = (nsub - 1) * 128 + Ls if nsub > 1 else Ls
        for si in range(nsub):
            Lc = 128 if nsub > 1 else Ls
            dma("sp", xf[0:Lc, si, :], srcs[si][0], [srcs[si][1]], [xf])
        for si in range(nsub):
            Lc = 128 if nsub > 1 else Ls
            rmsnorm_tile(xf, xf[0:Lc, si, :], Lc, gffn, hn[0:Lc, :], hn, 2)
            T0 = banks[0]; T0v = bf(T0)
            def tr8(e, Lc=Lc):
                for k in range(8):
                    ins = e.transpose(T0v[:, k * 128:k * 128 + Lc], hn[0:Lc, k * 128:(k + 1) * 128], identb[0:Lc, 0:Lc])
                return ins
            pe(tr8, [hn, identb], [T0])
            act(lambda e, si=si, Lc=Lc: e.activation(out=hfT[:, :, si * 128:si * 128 + Lc],
                                                     in_=T0v.rearrange("p (k t) -> p k t", k=8)[:, :, 0:Lc], func=AF.Copy), [T0], [hfT])

    def ffn_body(it):
        l, nsub, Ls, srcs, dsts, final, xf = it
        T = (nsub - 1) * 128 + Ls if nsub > 1 else Ls
        for j in range(NFF):
            bg_ = banks[1 + (j % 2)]; bu_ = banks[3 + (j % 2)]
            def gu(e, j=j, bg_=bg_, bu_=bu_):
                for k in range(8):
                    e.matmul(bg_.t[:, 0:T], lhsT=wg_sb[:, k, j * 128:(j + 1) * 128], rhs=hfT[:, k, 0:T], start=(k == 0), stop=(k == 7))
                for k in range(8):
                    ins = e.matmul(bu_.t[:, 0:T], lhsT=wu_sb[:, k, j * 128:(j + 1) * 128], rhs=hfT[:, k, 0:T], start=(k == 0), stop=(k == 7))
                return ins
            pe(gu, [WF[[gi for gi, (j0, j1) in enumerate(FFN_GROUPS) if j0 <= j < j1][0]], hfT], [bg_, bu_])
            t_ = tgt[j % 2]
            act(lambda e, t_=t_, bg_=bg_: e.activation(out=t_[:, 0:T], in_=bg_.t[:, 0:T], func=AF.Silu), [bg_], [t_])
            dve(lambda e, t_=t_, bu_=bu_, j=j: e.tensor_tensor(out=hhT[:, j, 0:T], in0=t_[:, 0:T], in1=bu_.t[:, 0:T], op=ALU.mult), [t_, bu_], [hhT])

    def ffn_tail(it):
        l, nsub, Ls, srcs, dsts, final, xf = it
        for si in range(nsub):
            Lc = 128 if nsub > 1 else Ls
            for half in range(2):
                bk = banks[5 + half]
                def dn(e, si=si, half=half, bk=bk, Lc=Lc):
                    for j in range(NFF):
                        ins = e.matmul(bk.t[0:Lc, :], lhsT=hhT[:, j, si * 128:si * 128 + Lc], rhs=wd_sb[:, j, half * 512:(half + 1) * 512],
                                       start=(j == 0), stop=(j == NFF - 1))
                    return ins
                pe(dn, [hhT, WB], [bk])
                dve(lambda e, si=si, half=half, bk=bk, Lc=Lc: e.scalar_tensor_tensor(
                    out=xf[0:Lc, si, half * 512:(half + 1) * 512], in0=bk.t[0:Lc, :], scalar=1.0, in1=xf[0:Lc, si, half * 512:(half + 1) * 512],
                    op0=ALU.mult, op1=ALU.add), [bk, xf], [xf])
            if final:
                y_ = yo[si % 2]
                dve(lambda e, Lc=Lc: e.memset(ss[0:Lc, 3:4], 0.0), [], [ss])
                act(lambda e, si=si, Lc=Lc: e.activation(out=junk[0:Lc, :], in_=xf[0:Lc, si, :], func=AF.Square, accum_out=ss[0:Lc, 3:4]), [xf, ss], [junk, ss])
                rsqrt_col(rs[0:Lc, 3:4], ss[0:Lc, 3:4], (rs, ss), 1.0, D * EPS)
                dve(lambda e, si=si, Lc=Lc, y_=y_: e.scalar_tensor_tensor(out=y_[0:Lc, :], in0=xf[0:Lc, si, :], scalar=rs[0:Lc, 3:4], in1=gfin[0:Lc, :],
                                                                          op0=ALU.mult, op1=ALU.mult), [xf, rs, gfin], [y_])
                dma("sp", dsts[si][0], y_[0:Lc, :], [y_], [dsts[si][1]] if dsts[si][1] is not None else [])
            else:
                dma("sp", dsts[si][0], xf[0:Lc, si, :], [xf], [dsts[si][1]])

    import os
    KSTOP = int(os.environ.get("KSTOP", "0"))

    class _Stop(Exception):
        pass

    def cut(n):
        if KSTOP == n:
            raise _Stop()

    KSUB = int(os.environ.get("KSUB", "0"))
    KTILE = int(os.environ.get("KTILE", "0"))

    def cutt(n, ti):
        if KSTOP == n and ti == KTILE:
            raise _Stop()

    def cutn(n, ti):
        if KSUB == n and ti == KTILE:
            raise _Stop()

    def schedule():
        for l in range(DEPTH):
            P.fence()
            load_layer_params(l)
            cut(1)
            load_mixer_weights(l)
            cut(2)
            init_prompt_ctx()
            src_p, src_s = (xp, xs) if l == 0 else (xb_p, xb_s)
            tiles = []
            for i in range(NPT):
                sbuf_ = None if l == 0 else dB(("xb_p", i))
                tiles.append(dict(c=ctxP, L=128, src=src_p[i * 128:(i + 1) * 128, :], sb=sbuf_, dst=xa_p[i * 128:(i + 1) * 128, :],
                                  db=dB(("xa_p", i)), kind="p", s=0, last=(i == NPT - 1)))
            for s in range(4):
                sbuf_ = None if l == 0 else dB(("xb_s", 0))
                tiles.append(dict(c=ctxS, L=16, src=src_s[s * 16:(s + 1) * 16, :], sb=sbuf_, dst=xa_s[s * 16:(s + 1) * 16, :],
                                  db=dB(("xa_s", 0)), kind="s", s=s, last=True))
            mixer_head_norm(tiles[0]["L"], tiles[0]["src"], tiles[0]["sb"], 0)
            mixer_head_tr(tiles[0]["L"], 0)
            for ti, t in enumerate(tiles):
                if t["kind"] == "s":
                    if t["s"] == 0:
                        cut(12)
                    init_sample_ctx(l, t["s"])
                    cut(13)
                nxt = tiles[ti + 1] if ti + 1 < len(tiles) else None
                hook = (lambda nxt=nxt, ti=ti: mixer_head_tr(nxt["L"], ti + 1)) if nxt is not None else None
                hook0 = (lambda nxt=nxt, ti=ti: mixer_head_norm(nxt["L"], nxt["src"], nxt["sb"], ti + 1)) if nxt is not None else None
                hoist = nxt is not None and nxt["kind"] == "p"
                hookA = (lambda nxt=nxt, ti=ti: mixer_front_A(nxt["c"], nxt["L"], ti + 1)) if hoist else None
                mixer_tile(l, t["c"], t["L"], t["src"], t["sb"], t["dst"], t["db"], t["kind"], t["s"], t["last"], ti, hook, hook0,
                           skip_A=(ti > 0 and t["kind"] == "p"), next_A=hookA)
                if t["kind"] == "s":
                    cut(14)
            cut(15)
            P.fence()
            load_ffn_weights(l)
            cut(16)
            final = (l == DEPTH - 1)
            NSUB = FT // 128
            nmt = (NPT + NSUB - 1) // NSUB
            items = []
            for m in range(nmt):
                subs = list(range(m * NSUB, min(NPT, m * NSUB + NSUB)))
                srcs = [(xa_p[i * 128:(i + 1) * 128, :], dB(("xa_p", i))) for i in subs]
                if final:
                    dsts = [(yp[i * 128:(i + 1) * 128, :], None) for i in subs]
                else:
                    dsts = [(xb_p[i * 128:(i + 1) * 128, :], dB(("xb_p", i))) for i in subs]
                items.append((l, len(subs), 128, srcs, dsts, final, xf2[len(items) % 2]))
            dsts = [(ys[:, :], None)] if final else [(xb_s[:, :], dB(("xb_s", 0)))]
            items.append((l, 1, 64, [(xa_s[:, :], dB(("xa_s", 0)))], dsts, final, xf2[len(items) % 2]))
            ffn_head(items[0])
            for ii, it in enumerate(items):
                ffn_body(it)
                if ii + 1 < len(items):
                    ffn_head(items[ii + 1])
                ffn_tail(it)
                cut(17)
            cut(18)


    try:
        schedule()
    except _Stop:
        P.fence()
        dma("sp", yp[0:1, 0:8], ss[0:1, 0:8], [], [])

    P.finalize(lambda n: es.enter_context(nc.semaphore(n)))
    with nc.Block() as block:
        @block.tensor
        def _(e):
            P.run_engine("pe", e)

        @block.scalar
        def _(e):
            P.run_engine("act", e)

        @block.vector
        def _(e):
            P.run_engine("dve", e)

        @block.gpsimd
        def _(e):
            P.run_engine("pool", e)

        @block.sync
        def _(e):
            P.run_engine("sp", e, last=True)
    build.sbuf_left = nc.sbuf_bytes_remaining
    es.close()
    return nc, P


_CACHE = {}


def _run(inputs, NPT, DEPTH=2, ncores=8, trace=False):
    key = (NPT, DEPTH)
    if key not in _CACHE:
        _CACHE[key] = build(NPT, DEPTH)[0]
    nc = _CACHE[key]
    f = lambda a: np.ascontiguousarray(np.asarray(a, dtype=np.float32))
    I = {k: f(v) for k, v in inputs.items()}
    consts = _consts()
    pfm, prow = _pack_params(I["ssm_conv_w"], I["ssm_conv_b"], I["ssm_dt_bias"], I["ssm_a_log"], I["ssm_d"], I["ssm_norm"],
                             I["swa_sinks"], I["sconv_w"], I["cconv_w"], I["cconv_b"], I["cconv_ln_g"], I["cconv_ln_b"])
    in_maps = []
    for c in range(ncores):
        b = c % 4
        sl = slice(4 * c, 4 * c + 4)
        m = {
            "xp": f(I["x_prompt"][b]), "xs": f(I["x_sample"][sl].reshape(64, D)),
            "st_ssm": f(I["state_ssm"][:, sl].reshape(DEPTH, 4, 256, 128)),
            "st_xc": f(I["state_ssm_conv"][:, sl]),
            "ck": f(I["cache_swa_k"][:, sl].reshape(DEPTH, 4, 128, 128)),
            "cv": f(I["cache_swa_v"][:, sl].reshape(DEPTH, 4, 128, 128)),
            "st_sc": f(I["state_sconv"][:, sl]), "st_cc": f(I["state_cconv"][:, sl]),
            "w_in": I["w_in"], "w_out": I["w_out"], "w_gate": I["w_gate"], "w_up": I["w_up"], "w_down": I["w_down"],
            "norm_mix": I["norm_mix"], "norm_ffn": I["norm_ffn"], "norm_final": f(I["norm_final"].reshape(1, D)),
            "pfm": pfm, "prow": prow, "ssm_conv_b": I["ssm_conv_b"],
        }
        m.update(consts)
        in_maps.append(m)
    res = run_bass_kernel_spmd(nc, in_maps, core_ids=list(range(ncores)))
    R = res.results
    nb = 4
    S = NPT * 128
    y_prompt = np.stack([R[b]["yp"] for b in range(nb)], 0)
    y_sample = np.concatenate([R[c]["ys"].reshape(4, 16, D) for c in range(ncores)], 0)

    def pstack(name, shp):
        return np.stack([R[b][name].reshape((DEPTH,) + shp) for b in range(nb)], 1)

    def sstack(name, shp):
        return np.concatenate([R[c][name].reshape((DEPTH, 4) + shp) for c in range(ncores)], 1)

    outs = (y_prompt, y_sample,
            pstack("o_ssm_p", (4, 64, 128)), sstack("o_ssm_s", (4, 64, 128)),
            pstack("o_xc_p", (3, 512)), sstack("o_xc_s", (3, 512)),
            pstack("o_k_p", (128, 2, 64)), sstack("o_k_s", (16, 2, 64)),
            pstack("o_v_p", (128, 2, 64)), sstack("o_v_s", (16, 2, 64)),
            pstack("o_sc_p", (2, 256)), sstack("o_sc_s", (2, 256)),
            pstack("o_cc_p", (30, 256)), sstack("o_cc_s", (30, 256)))
    return tuple(np.ascontiguousarray(o.astype(np.float32)) for o in outs)


def kernel(**inputs):
    return _run(inputs, NPT=32, DEPTH=2)
```
